# Optimizing a Trainium2 kernel written in Bass

```python
import jax, jax.numpy as jnp
from jax import lax
import numpy as np

D_MODEL = 1024
BATCH = 2
SEQ = 8192
DEPTH = 2
DEC_BATCH = 128
DEC_SEQ = 1
PAST_LEN = 16384
PAGE_SIZE = 128

W_POOL = D_MODEL // 2
W_CONV = D_MODEL // 2
POOL_WINDOWS = (2, 4, 8, 16)
N_POOL_GROUPS = len(POOL_WINDOWS)
POOL_GROUP = W_POOL // N_POOL_GROUPS
POOL_BUF = max(POOL_WINDOWS) - 1
CONV_WIDTH = 3
CONV_BUF = CONV_WIDTH - 1
IN0_COLS = 2 * W_POOL + 4 * W_CONV
EVEN_SPLITS = (W_POOL, 2 * W_POOL, 2 * W_POOL + W_CONV, 2 * W_POOL + 2 * W_CONV, 2 * W_POOL + 3 * W_CONV)
N_HEADS = 16
HEAD_DIM = 64
N_KV_HEADS = 4
GROUP = N_HEADS // N_KV_HEADS
WINDOW = 128
BLOCK = 128
ROPE_THETA = 10000.0
ATTN_W = N_HEADS * HEAD_DIM
KV_W = N_KV_HEADS * HEAD_DIM
IN1_COLS = 2 * ATTN_W + 2 * KV_W
ODD_SPLITS = (ATTN_W, ATTN_W + KV_W, ATTN_W + 2 * KV_W)
RMS_EPS = 1e-6
N_EVEN = (DEPTH + 1) // 2
N_ODD = DEPTH // 2

kernel_name = "hybrid_pool_conv_swa_decoder_step"


def rms_norm(x, g):
    xf = x.astype(jnp.float32)
    y = xf * lax.rsqrt(jnp.mean(xf * xf, axis=-1, keepdims=True) + RMS_EPS)
    return (y * g.astype(jnp.float32)).astype(x.dtype)


def rope(x, pos):
    half = HEAD_DIM // 2
    inv = ROPE_THETA ** (-jnp.arange(half, dtype=jnp.float32) / half)
    ang = pos.astype(jnp.float32)[:, None] * inv[None, :]
    cos = jnp.cos(ang)[:, None, :]
    sin = jnp.sin(ang)[:, None, :]
    xf = x.astype(jnp.float32)
    x1, x2 = xf[..., :half], xf[..., half:]
    return jnp.concatenate([x1 * cos - x2 * sin, x2 * cos + x1 * sin], axis=-1).astype(x.dtype)


def pool_mix(u, prefix, pos0):
    T = u.shape[1]
    ext = jnp.concatenate([prefix, u], axis=1).astype(jnp.float32)
    cs = jnp.pad(jnp.cumsum(ext, axis=1), ((0, 0), (1, 0), (0, 0)))
    pos = pos0 + jnp.arange(T)
    ends = cs[:, POOL_BUF + 1:POOL_BUF + 1 + T]
    outs = []
    for g, w in enumerate(POOL_WINDOWS):
        lo, hi = g * POOL_GROUP, (g + 1) * POOL_GROUP
        start = cs[:, POOL_BUF + 1 - w:POOL_BUF + 1 - w + T, lo:hi]
        cnt = jnp.minimum(pos + 1, w).astype(jnp.float32)[None, :, None]
        outs.append((ends[..., lo:hi] - start) / cnt)
    mean = jnp.concatenate(outs, axis=-1)
    return (mean - u.astype(jnp.float32)).astype(u.dtype)


def short_conv(v, prefix, conv_w):
    T = v.shape[1]
    ext = jnp.concatenate([prefix, v], axis=1)
    return sum(ext[:, k:k + T] * conv_w[k] for k in range(CONV_WIDTH))


def even_layer(x, pool_prefix, conv_prefix, pos0, g_norm, w_in, pool_w, pool_scale, conv_w, w_out):
    B, T = x.shape[0], x.shape[1]
    h = rms_norm(x, g_norm)
    p = jnp.einsum('btd,de->bte', h, w_in)
    u, z_a, b_gate, c_gate, v_in, z_b = jnp.split(p, EVEN_SPLITS, axis=-1)
    d = pool_mix(u, pool_prefix, pos0).reshape(B, T, N_POOL_GROUPS, POOL_GROUP)
    y_a = jnp.einsum('btgc,gce->btge', d, pool_w).reshape(B, T, W_POOL) * pool_scale * jax.nn.silu(z_a)
    v = c_gate * v_in
    y_b = b_gate * short_conv(v, conv_prefix, conv_w) * jax.nn.silu(z_b)
    y = jnp.einsum('bte,ed->btd', jnp.concatenate([y_a, y_b], axis=-1), w_out)
    new_pool = jnp.concatenate([pool_prefix, u], axis=1)[:, -POOL_BUF:]
    new_conv = jnp.concatenate([conv_prefix, v], axis=1)[:, -CONV_BUF:]
    return x + y, new_pool, new_conv


def qkv_proj(x, g_norm, w_in, q_gain, k_gain, pos):
    B, T = x.shape[0], x.shape[1]
    h = rms_norm(x, g_norm)
    p = jnp.einsum('btd,de->bte', h, w_in)
    q, k, v, z = jnp.split(p, ODD_SPLITS, axis=-1)
    q = rope(rms_norm(q.reshape(B, T, N_HEADS, HEAD_DIM), q_gain), pos)
    k = rope(rms_norm(k.reshape(B, T, N_KV_HEADS, HEAD_DIM), k_gain), pos)
    v = v.reshape(B, T, N_KV_HEADS, HEAD_DIM)
    return q, k, v, z


def sink_softmax(s, mask, sinks):
    sk = sinks.astype(jnp.float32).reshape(N_KV_HEADS, GROUP, 1, 1)
    s = jnp.where(mask, s, -jnp.inf)
    m = jnp.maximum(jnp.max(s, axis=-1, keepdims=True), sk)
    p = jnp.exp(s - m)
    return p / (jnp.sum(p, axis=-1, keepdims=True) + jnp.exp(sk - m))


def swa_prompt(q, k, v, sinks):
    B, T = q.shape[0], q.shape[1]
    nb = T // BLOCK
    qb = q.reshape(B, nb, BLOCK, N_KV_HEADS, GROUP, HEAD_DIM)
    kb = k.reshape(B, nb, BLOCK, N_KV_HEADS, HEAD_DIM)
    vb = v.reshape(B, nb, BLOCK, N_KV_HEADS, HEAD_DIM)
    pad = ((0, 0), (1, 0), (0, 0), (0, 0), (0, 0))
    kk = jnp.concatenate([jnp.pad(kb, pad)[:, :-1], kb], axis=2)
    vv = jnp.concatenate([jnp.pad(vb, pad)[:, :-1], vb], axis=2)
    s = jnp.einsum('bnqhgd,bnkhd->bnhgqk', qb, kk, preferred_element_type=jnp.float32) * (HEAD_DIM ** -0.5)
    qi = jnp.arange(BLOCK)[:, None] + BLOCK
    ki = jnp.arange(2 * BLOCK)[None, :]
    band = (ki <= qi) & (qi - ki < WINDOW)
    valid = (jnp.arange(nb)[:, None, None] > 0) | (ki[None] >= BLOCK)
    mask = (band[None] & valid)[None, :, None, None]
    pr = sink_softmax(s, mask, sinks).astype(v.dtype)
    o = jnp.einsum('bnhgqk,bnkhd->bnqhgd', pr, vv)
    return o.reshape(B, T, ATTN_W)


def swa_sample(q, k, v, k_buf, v_buf, sinks, pos0):
    B, T = q.shape[0], q.shape[1]
    P = k_buf.shape[1]
    kk = jnp.concatenate([k_buf, k], axis=1)
    vv = jnp.concatenate([v_buf, v], axis=1)
    qg = q.reshape(B, T, N_KV_HEADS, GROUP, HEAD_DIM)
    s = jnp.einsum('bqhgd,bkhd->bhgqk', qg, kk, preferred_element_type=jnp.float32) * (HEAD_DIM ** -0.5)
    qpos = pos0 + jnp.arange(T)[:, None]
    kpos = pos0 - P + jnp.arange(P + T)[None, :]
    mask = ((kpos <= qpos) & (qpos - kpos < WINDOW))[None, None, None]
    pr = sink_softmax(s, mask, sinks).astype(v.dtype)
    o = jnp.einsum('bhgqk,bkhd->bqhgd', pr, vv).reshape(B, T, ATTN_W)
    return o, kk[:, -P:], vv[:, -P:]


def setup_inputs(seed: int = 0) -> dict:
    key = jax.random.key(seed)
    ks = jax.random.split(key, 20)
    win_buf = min(WINDOW, PAST_LEN)
    nrm = jax.random.normal
    return {
        "x_prompt": nrm(ks[0], (BATCH, SEQ, D_MODEL), jnp.float32),
        "x_sample": nrm(ks[1], (DEC_BATCH, DEC_SEQ, D_MODEL), jnp.float32),
        "state_pool": nrm(ks[2], (N_EVEN, DEC_BATCH, POOL_BUF, W_POOL), jnp.float32),
        "state_conv": nrm(ks[3], (N_EVEN, DEC_BATCH, CONV_BUF, W_CONV), jnp.float32),
        "cache_k": nrm(ks[4], (N_ODD, DEC_BATCH, win_buf, N_KV_HEADS, HEAD_DIM), jnp.float32),
        "cache_v": nrm(ks[5], (N_ODD, DEC_BATCH, win_buf, N_KV_HEADS, HEAD_DIM), jnp.float32),
        "norm_g": 1.0 + 0.02 * nrm(ks[6], (DEPTH, D_MODEL), jnp.float32),
        "w_in_even": nrm(ks[7], (N_EVEN, D_MODEL, IN0_COLS), jnp.float32) * D_MODEL ** -0.5,
        "pool_w": nrm(ks[8], (N_EVEN, N_POOL_GROUPS, POOL_GROUP, POOL_GROUP), jnp.float32) * POOL_GROUP ** -0.5,
        "pool_scale": 1.0 + 0.02 * nrm(ks[9], (N_EVEN, W_POOL), jnp.float32),
        "conv_w": nrm(ks[10], (N_EVEN, CONV_WIDTH, W_CONV), jnp.float32) * CONV_WIDTH ** -0.5,
        "w_out_even": nrm(ks[11], (N_EVEN, W_POOL + W_CONV, D_MODEL), jnp.float32) * (W_POOL + W_CONV) ** -0.5,
        "w_in_odd": nrm(ks[12], (N_ODD, D_MODEL, IN1_COLS), jnp.float32) * D_MODEL ** -0.5,
        "q_norm_g": 1.0 + 0.02 * nrm(ks[13], (N_ODD, HEAD_DIM), jnp.float32),
        "k_norm_g": 1.0 + 0.02 * nrm(ks[14], (N_ODD, HEAD_DIM), jnp.float32),
        "attn_sinks": 0.5 * nrm(ks[15], (N_ODD, N_HEADS), jnp.float32),
        "w_out_odd": nrm(ks[16], (N_ODD, ATTN_W, D_MODEL), jnp.float32) * ATTN_W ** -0.5,
    }


def reference(x_prompt, x_sample, state_pool, state_conv, cache_k, cache_v, norm_g, w_in_even, pool_w,
              pool_scale, conv_w, w_out_even, w_in_odd, q_norm_g, k_norm_g, attn_sinks, w_out_odd):
    xp, xs = x_prompt, x_sample
    Bp, Tp = xp.shape[0], xp.shape[1]
    Ts = xs.shape[1]
    win_buf = cache_k.shape[2]
    pos_p = jnp.arange(Tp)
    pos_s = PAST_LEN + jnp.arange(Ts)
    pool_p, pool_s, conv_p, conv_s = [], [], [], []
    kp_l, vp_l, ks_l, vs_l = [], [], [], []
    for layer in range(DEPTH):
        i = layer // 2
        if layer % 2 == 0:
            wts = (norm_g[layer], w_in_even[i], pool_w[i], pool_scale[i], conv_w[i], w_out_even[i])
            zp = jnp.zeros((Bp, POOL_BUF, W_POOL), xp.dtype)
            zc = jnp.zeros((Bp, CONV_BUF, W_CONV), xp.dtype)
            xp, np_, nc_ = even_layer(xp, zp, zc, 0, *wts)
            xs, ns_, ncs_ = even_layer(xs, state_pool[i], state_conv[i], PAST_LEN, *wts)
            pool_p.append(np_); conv_p.append(nc_); pool_s.append(ns_); conv_s.append(ncs_)
        else:
            q, k, v, z = qkv_proj(xp, norm_g[layer], w_in_odd[i], q_norm_g[i], k_norm_g[i], pos_p)
            o = swa_prompt(q, k, v, attn_sinks[i])
            xp = xp + jnp.einsum('bte,ed->btd', o * jax.nn.silu(z), w_out_odd[i])
            kp_l.append(k[:, -win_buf:]); vp_l.append(v[:, -win_buf:])
            q, k, v, z = qkv_proj(xs, norm_g[layer], w_in_odd[i], q_norm_g[i], k_norm_g[i], pos_s)
            o, nk, nv = swa_sample(q, k, v, cache_k[i], cache_v[i], attn_sinks[i], PAST_LEN)
            xs = xs + jnp.einsum('bte,ed->btd', o * jax.nn.silu(z), w_out_odd[i])
            ks_l.append(nk); vs_l.append(nv)
    return (xp, xs, jnp.stack(pool_p), jnp.stack(pool_s), jnp.stack(conv_p), jnp.stack(conv_s),
            jnp.stack(kp_l), jnp.stack(vp_l), jnp.stack(ks_l), jnp.stack(vs_l))
```

```python
import contextlib
import math
import os
import numpy as np
import concourse.bass as bass
import concourse.mybir as mybir
from concourse.bass_utils import run_bass_kernel_spmd

F32 = mybir.dt.float32
BF16 = mybir.dt.bfloat16
I32 = mybir.dt.int32
ALU = mybir.AluOpType
AF = mybir.ActivationFunctionType

ENG = ["pe", "act", "dve", "pool", "sp"]
PAST_LEN = 16384
RMS_EPS = 1e-6


class Sched:
    def __init__(self, nc, stack):
        self.nc = nc
        self.stack = stack
        self.prog = {e: [] for e in ENG}
        self.sems = {}
        self.cnt = {}
        self.seen = {e: {} for e in ENG}
        self.lastw = {}
        self.readers = {}
        self.group_sems = {"c0", "c1", "c2", "wgs0", "wgs1", "wgs2", "wOs", "wscrA", "wscrB", "wscrC", "wnatA", "tabs"}
        for e in ENG:
            self._sem("E_" + e)

    def _sem(self, name):
        if name not in self.sems:
            self.sems[name] = self.stack.enter_context(self.nc.semaphore(name))
            self.cnt[name] = 0
        return self.sems[name]

    def _deps(self, reads, writes):
        deps = {}

        def add(d):
            if d is not None:
                s, v = d
                if deps.get(s, 0) < v:
                    deps[s] = v

        for k in reads:
            add(self.lastw.get(k))
            if k.startswith("ps"):
                for d in self.readers.get(k, ()):
                    add(d)
        for k in writes:
            add(self.lastw.get(k))
            for d in self.readers.get(k, ()):
                add(d)
        return deps

    def _waits(self, eng, deps):
        own = "E_" + eng
        for s, v in deps.items():
            if eng == "pe" and s == own:
                continue
            if s in self.group_sems:
                v = self.cnt[s]
            if self.seen[eng].get(s, 0) >= v:
                continue
            self.seen[eng][s] = v
            self.prog[eng].append(("wait", s, v))

    def _record(self, d, reads, writes):
        for k in writes:
            self.lastw[k] = d
            self.readers[k] = []
        for k in reads:
            self.readers.setdefault(k, []).append(d)

    def op(self, eng, fn, reads=(), writes=(), signal=True):
        self._waits(eng, self._deps(reads, writes))
        own = "E_" + eng
        if signal:
            self.cnt[own] += 1
            tick = self.cnt[own]
        else:
            tick = self.cnt[own] + 1
        self.prog[eng].append(("op", fn, own if signal else None))
        self._record((own, tick), reads, writes)

    def dma(self, q, out, in_, reads=(), writes=(), sem="dma", **kw):
        self._sem(sem)
        self._waits(q, self._deps(reads, writes))
        self.cnt[sem] += 16
        self.prog[q].append(("dma", out, in_, sem, kw))
        self._record((sem, self.cnt[sem]), reads, writes)

    def wait_all(self, eng, sems):
        for s in sems:
            v = self.cnt[s]
            if v > 0 and self.seen[eng].get(s, 0) < v:
                self.seen[eng][s] = v
                self.prog[eng].append(("wait", s, v))

    def barrier(self):
        names = list(self.sems.keys())
        for e in ENG:
            self.wait_all(e, [n for n in names if not (e == "pe" and n == "E_pe")])

    def replay(self):
        sems = self.sems

        def run(eng_obj, items):
            for it in items:
                if it[0] == "wait":
                    eng_obj.wait_ge(sems[it[1]], it[2])
                elif it[0] == "op":
                    ins = it[1](eng_obj)
                    if it[2] is not None:
                        ins.then_inc(sems[it[2]], 1)
                else:
                    _, out, in_, s, kw = it
                    eng_obj.dma_start(out=out, in_=in_, **kw).then_inc(sems[s], 16)

        with self.nc.Block() as block:
            @block.tensor
            def _(e):
                run(e, self.prog["pe"])

            @block.scalar
            def _(e):
                run(e, self.prog["act"])

            @block.vector
            def _(e):
                run(e, self.prog["dve"])

            @block.gpsimd
            def _(e):
                run(e, self.prog["pool"])

            @block.sync
            def _(e):
                run(e, self.prog["sp"])


def build(NMT=4, stop_after_l0=False, stop_at=None):
    NT = 4 * NMT
    NS = 144 + 128 * NT
    nc = bass.Bass("TRN2", target_bir_lowering=False)

    def din(n, s):
        return nc.dram_tensor(n, list(s), F32, kind="ExternalInput").ap()

    def dout(n, s):
        return nc.dram_tensor(n, list(s), F32, kind="ExternalOutput").ap()

    xin = din("xin", [NS, 1024]); xs_d = din("xs", [16, 1024])
    spool_d = din("spool", [240, 512]); sconv_d = din("sconv", [32, 512])
    ck_d = din("ck", [16, 128, 256]); cv_d = din("cv", [16, 128, 256])
    ng_d = din("norm_g", [2, 1024]); wie_d = din("w_in_even", [1024, 3072])
    pw_d = din("pool_w", [4, 128, 128]); psc_d = din("pool_scale", [1, 512])
    cw_d = din("conv_w", [3, 512]); woe_d = din("w_out_even", [1024, 1024])
    wio_d = din("w_in_odd", [1024, 2560]); qg_d = din("q_norm_g", [1, 64]); kg_d = din("k_norm_g", [1, 64])
    sk_d = din("attn_sinks", [1, 16]); woo_d = din("w_out_odd", [1024, 1024]); meta_d = din("meta", [1, 2])

    y_d = dout("y", [128 * NT, 1024]); ys_d = dout("ys", [16, 1024])
    npp_d = dout("npp", [15, 512]); nps_d = dout("nps", [16, 15, 512])
    ncp_d = dout("ncp", [2, 512]); ncs_d = dout("ncs", [16, 2, 512])
    nkp_d = dout("nkp", [128, 256]); nvp_d = dout("nvp", [128, 256])
    nks_d = dout("nks", [16, 128, 256]); nvs_d = dout("nvs", [16, 128, 256])
    wscr = nc.dram_tensor("wscr", [28, 128, 1024], BF16).ap()
    tabs = nc.dram_tensor("tabs", [2, 128, NS + 16], F32).ap()
    wnat = nc.dram_tensor("wnat", [1024, 2560], BF16).ap()
    wonat = nc.dram_tensor("wonat", [1024, 1024], BF16).ap()

    with contextlib.ExitStack() as stack:
        S = Sched(nc, stack)

        def sb(n, s, d=F32):
            return stack.enter_context(nc.sbuf_tensor(n, list(s), d))

        x1s = sb("x1s", [128, NT + 1, 1024])
        xsm = sb("xsm", [16, 1024]); xpre = sb("xpre", [128, 1024])
        wO = sb("wO", [128, 8, 1024], BF16)
        stg = sb("stg", [128, 2, 1024])
        gt = sb("gt", [128, 1024])
        hT = sb("hT", [128, 8, 512], BF16)
        hn = sb("hn", [128, 2, 1024], BF16)
        id16 = sb("id16", [128, 128], BF16); idf = sb("idf", [128, 128])
        ssq = sb("ssq", [128, 4]); ms = sb("ms", [128, 4]); rstd = sb("rstd", [128, 4]); nhalf = sb("nhalf", [128, 4])
        metab = sb("metab", [128, 2])
        perm16 = sb("perm16", [128, 128], BF16); blk64 = sb("blk64", [128, 128], BF16); ones16 = sb("ones16", [128, 128], BF16)
        gq2 = sb("gq2", [128, 2]); skb = sb("skb", [128, 16]); ske = sb("ske", [128, 16])
        sgn = sb("sgn", [128, 1]); invf = sb("invf", [128, 1]); expo = sb("expo", [128, 1]); base1e4 = sb("base1e4", [128, 1])
        vf32 = sb("vf32", [128, 256])
        epsb = sb("epsb", [128, 1]); oneb = sb("oneb", [128, 1])
        gsm = sb("gsm", [128, 16])
        tmpi = sb("tmpi", [128, 512], I32)
        SCRW = 22200
        scr = sb("scr", [128, SCRW])
        psbig = stack.enter_context(nc.psum_tensor("psbig", [128, 4096], F32))
        ps = [psbig[:, i * 512:(i + 1) * 512] for i in range(8)]

        class Carver:
            def __init__(self):
                self.off = 0

            def alloc(self, shape, dt=F32):
                n = int(np.prod(shape))
                nb = n * (4 if dt in (F32, I32) else 2)
                nb4 = (nb + 3) // 4
                assert self.off + nb4 <= SCRW, ("scratch overflow", self.off + nb4, SCRW)
                ap = scr[:, self.off:self.off + nb4]
                self.off += nb4
                if dt != F32:
                    ap = ap.bitcast(dt)
                if dt == BF16 and nb % 4:
                    ap = ap[:, 0:n]
                if len(shape) == 2:
                    return ap.rearrange("p (a b) -> p a b", b=shape[1])
                if len(shape) == 3:
                    return ap.rearrange("p (a b c) -> p a b c", b=shape[1], c=shape[2])
                return ap

        bank_rr = [0]

        held = set()

        def bank(hold=False):
            for k in range(8):
                i = (bank_rr[0] + k) % 8
                if i not in held:
                    bank_rr[0] = (i + 1) % 8
                    if hold:
                        held.add(i)
                    return i
            raise RuntimeError("no free PSUM bank")

        def release(i):
            held.discard(i)

        def bank_pair():
            for k in range(4):
                i = (2 * ((bank_rr[0] + 1) // 2) + 2 * k) % 8
                if i not in held and (i + 1) not in held:
                    held.add(i); held.add(i + 1)
                    bank_rr[0] = (i + 2) % 8
                    return i
            raise RuntimeError("no free PSUM bank pair")

        def mm_group(out_ap, pairs, reads, bkey):
            n = len(pairs)
            for i, (l, r) in enumerate(pairs):
                S.op("pe", (lambda e, l=l, r=r, i=i: e.matmul(out_ap, lhsT=l, rhs=r, start=(i == 0), stop=(i == n - 1))),
                     reads=reads, writes=[bkey], signal=(i == n - 1))

        S.op("pool", lambda e: e.iota(tmpi[:, 0:128], pattern=[[1, 128]], base=0, channel_multiplier=-1), writes=["tmpi"])
        S.op("dve", lambda e: e.tensor_copy(out=idf[:], in_=tmpi[:, 0:128]), reads=["tmpi"], writes=["idf"])
        S.op("dve", lambda e: e.tensor_single_scalar(out=id16[:], in_=idf[:], scalar=0.0, op=ALU.is_equal), reads=["idf"], writes=["id16"])
        S.op("dve", lambda e: e.tensor_single_scalar(out=idf[:], in_=idf[:], scalar=0.0, op=ALU.is_equal), reads=["idf"], writes=["idf"])
        S.op("pool", lambda e: e.memset(nhalf[:], -0.5), writes=["nhalf"])
        S.op("pool", lambda e: e.memset(ssq[:], 1.0), writes=["ssq0", "ssq1", "ssq2", "ssq3"])
        S.dma("sp", metab[:], meta_d[0:1, :].partition_broadcast(128), writes=["metab"], sem="c0")
        S.dma("sp", gsm[:, 0:4], psc_d[0, :].rearrange("(g p) -> p g", p=128), writes=["gsm"], sem="c0", allow_slow_non_contiguous=True)
        S.dma("sp", gsm[:, 4:16].rearrange("p (k g) -> p k g", g=4), cw_d.rearrange("k (g p) -> p k g", p=128), writes=["gsm"], sem="c0", allow_slow_non_contiguous=True)
        S.dma("sp", gt[:], ng_d[0:1, :].partition_broadcast(128), writes=["gt"], sem="c0")
        for half in range(2):
            S.dma("sp", gq2[half * 64:(half + 1) * 64, 0:1], qg_d[0, :].rearrange("(d o) -> d o", o=1), writes=["gq2"], sem="c0", allow_slow_non_contiguous=True)
            S.dma("sp", gq2[half * 64:(half + 1) * 64, 1:2], kg_d[0, :].rearrange("(d o) -> d o", o=1), writes=["gq2"], sem="c0", allow_slow_non_contiguous=True)
        S.dma("sp", skb[:], sk_d[0:1, :].partition_broadcast(128), writes=["skb"], sem="c0")
        for q4 in range(4):
            tgt = 32.0 if q4 % 2 == 0 else -32.0
            S.op("dve", (lambda e, q4=q4, tgt=tgt: e.tensor_single_scalar(out=perm16[32 * q4:32 * q4 + 32, :], in_=tmpi[32 * q4:32 * q4 + 32, 0:128], scalar=tgt, op=ALU.is_equal)),
                 reads=["tmpi"], writes=["perm16"])
        S.op("pool", lambda e: e.memset(blk64[:], 0.0), writes=["blk64"])
        S.op("pool", lambda e: e.memset(blk64[0:64, 0:64], 1.0), reads=["blk64"], writes=["blk64"])
        S.op("pool", lambda e: e.memset(blk64[64:128, 64:128], 1.0), reads=["blk64"], writes=["blk64"])
        S.op("pool", lambda e: e.memset(ones16[:], 1.0), writes=["ones16"])
        for q4 in range(4):
            S.op("pool", (lambda e, q4=q4: e.memset(sgn[32 * q4:32 * q4 + 32, :], -1.0 if q4 % 2 == 0 else 1.0)), writes=["sgn"])
        S.op("pool", lambda e: e.memset(base1e4[:], 10000.0), writes=["base1e4"])
        S.op("pool", lambda e: e.memset(epsb[:], RMS_EPS), writes=["epsb"])
        S.op("pool", lambda e: e.memset(oneb[:], 1.0), writes=["oneb"])
        S.op("pool", lambda e: e.iota(tmpi[:, 168:169], pattern=[[0, 1]], base=0, channel_multiplier=1), writes=["tmpi5"])
        S.op("dve", lambda e: e.tensor_single_scalar(out=tmpi[:, 168:169], in_=tmpi[:, 168:169], scalar=31, op=ALU.bitwise_and), reads=["tmpi5"], writes=["tmpi5"])
        S.op("dve", lambda e: e.tensor_scalar(out=expo[:], in0=tmpi[:, 168:169], scalar1=-1.0 / 32, scalar2=None, op0=ALU.mult), reads=["tmpi5"], writes=["expo"])
        S.op("pool", lambda e: e.tensor_tensor(out=invf[:], in0=base1e4[:], in1=expo[:], op=ALU.pow), reads=["base1e4", "expo"], writes=["invf"])

        C = Carver()
        wbig = C.alloc([8, 3072], BF16)
        poolw = C.alloc([4, 128], BF16)
        ubuf = C.alloc([2, 528]); vbuf = C.alloc([2, 516])
        ucar = C.alloc([4, 16]); vcar = C.alloc([4, 2])
        ycat = C.alloc([8, 512], BF16)
        tA = C.alloc([528])[:, :]; tB = C.alloc([528])[:, :]
        dT = C.alloc([2, 512], BF16)
        sza = C.alloc([1, 512]); ctmp = C.alloc([1, 512]); szb = C.alloc([1, 512]); acc = C.alloc([1, 512])
        _o = C.off
        cbf = C.alloc([2, 1024], BF16)
        C.off = _o
        spl = C.alloc([2, 512])
        rcw = C.alloc([4, 16]); selp = C.alloc([4, 8]); selc = C.alloc([2, 16])
        scv = C.alloc([512])
        lastT = C.alloc([4, 32])
        l0_end = C.off

        S.op("pool", lambda e: e.memset(ucar, 0.0), writes=["ucar0", "ucar1", "ucar2", "ucar3"])
        S.op("pool", lambda e: e.memset(vcar, 0.0), writes=["vcar0", "vcar1", "vcar2", "vcar3"])

        S.op("pool", lambda e: e.iota(tmpi[:, 128:144], pattern=[[1, 16]], base=1, channel_multiplier=0), writes=["tmpi2"])
        for g in range(4):
            w = float(2 ** (g + 1))
            S.op("dve", (lambda e, g=g: e.tensor_copy(out=rcw[:, g, :], in_=tmpi[:, 128:144])), reads=["tmpi2"], writes=["rcw"])
            S.op("dve", (lambda e, g=g, w=w: e.tensor_scalar(out=rcw[:, g, :], in0=rcw[:, g, :], scalar1=metab[:, 0:1], scalar2=w, op0=ALU.add, op1=ALU.min)),
                 reads=["rcw", "metab"], writes=["rcw"])
            S.op("dve", (lambda e, g=g: e.reciprocal(out=rcw[:, g, :], in_=rcw[:, g, :])), reads=["rcw"], writes=["rcw"])
            S.op("dve", (lambda e, g=g, w=w: e.tensor_scalar(out=rcw[:, g, :], in0=rcw[:, g, :], scalar1=w, scalar2=None, op0=ALU.mult)),
                 reads=["rcw"], writes=["rcw"])

        S.op("pool", lambda e: e.iota(tmpi[:, 144:152], pattern=[[-15, 8]], base=0, channel_multiplier=1), writes=["tmpi3"])
        for g in range(4):
            w = 2 ** (g + 1)
            S.op("dve", (lambda e, g=g: e.tensor_copy(out=selp[:, g, :], in_=tmpi[:, 144:152])), reads=["tmpi3", "selp"], writes=["selp"])
            S.op("dve", (lambda e, g=g, w=w: e.tensor_scalar(out=tA[:, 0:8], in0=selp[:, g, :], scalar1=float(16 - w), scalar2=None, op0=ALU.is_ge)),
                 reads=["selp"], writes=["tA"])
            S.op("dve", (lambda e, g=g: e.tensor_scalar(out=selp[:, g, :], in0=selp[:, g, :], scalar1=14.0, scalar2=None, op0=ALU.is_le)),
                 reads=["selp"], writes=["selp"])
            S.op("dve", (lambda e, g=g: e.tensor_tensor(out=selp[:, g, :], in0=selp[:, g, :], in1=tA[:, 0:8], op=ALU.mult)),
                 reads=["selp", "tA"], writes=["selp"])
        S.op("pool", lambda e: e.iota(tmpi[:, 152:168], pattern=[[-2, 16]], base=0, channel_multiplier=1), writes=["tmpi4"])
        for r in range(2):
            S.op("dve", (lambda e, r=r: e.tensor_copy(out=selc[:, r, :], in_=tmpi[:, 152:168])), reads=["tmpi4"], writes=["selc"])
            S.op("dve", (lambda e, r=r: e.tensor_scalar(out=selc[:, r, :], in0=selc[:, r, :], scalar1=float(r), scalar2=None, op0=ALU.is_equal)),
                 reads=["selc"], writes=["selc"])

        def xtile_ap(t):
            return xin[16 + 128 * t:16 + 128 * (t + 1), :]

        S.dma("sp", xpre[0:16, :], xin[0:16, :], writes=["xpre"], sem="xpre")
        S.dma("sp", x1s[:, 0, :], xtile_ap(0), writes=["x1_0"], sem="x1_0")
        S.dma("sp", xsm[:], xs_d[:, :], writes=["xsm"], sem="xsm")
        S.dma("sp", spl[0:120, 0, :], spool_d[0:120, :], writes=["cbf0"], sem="c1")
        S.dma("sp", spl[0:120, 1, :], spool_d[120:240, :], writes=["cbf1"], sem="c1")
        S.dma("sp", scv[0:32, :], sconv_d[:, :], writes=["scv"], sem="c1")

        stg_rr = [0]

        cast_rr = [0]

        def load_cast(dst, src, dkey, shape3=None):
            s = stg_rr[0]
            stg_rr[0] ^= 1
            view = stg[:, s, :]
            if shape3 is not None:
                view = view.rearrange("p (a b) -> p a b", b=shape3)
            S.dma("sp", view, src, writes=[f"stg{s}"], sem=f"stg{s}")
            cast_rr[0] ^= 1
            if cast_rr[0]:
                S.op("dve", lambda e: e.tensor_copy(out=dst, in_=view), reads=[f"stg{s}"], writes=[dkey])
            else:
                S.op("act", lambda e: e.activation(out=dst, in_=view, func=AF.Copy), reads=[f"stg{s}"], writes=[dkey])

        def load_pw():
            s = stg_rr[0]
            stg_rr[0] ^= 1
            view = stg[:, s, 0:512].rearrange("p (g e) -> p g e", e=128)
            S.dma("sp", view, pw_d.rearrange("g c e -> c g e"), writes=[f"stg{s}"], sem=f"stg{s}")
            S.op("pool", lambda e: e.tensor_copy(out=poolw, in_=view), reads=[f"stg{s}"], writes=["poolw"])

        fc_order = []
        for g in range(4):
            fc_order += [g, 4 + g]
        for i in range(4):
            fc_order += [12 + i, 16 + i, 8 + i, 20 + i]
        pend = []
        pend.append(lambda: S.dma("pool", poolw, pw_d.rearrange("g c e -> c g e"), writes=["poolw"], sem="pws"))
        xt_list = list(range(1, NT + 1))

        def load_wgroup(cg, kc):
            S.dma("pool", wbig[:, kc, cg * 1024:(cg + 1) * 1024], wie_d[kc * 128:(kc + 1) * 128, cg * 1024:(cg + 1) * 1024],
                  writes=[f"wg{cg}_{kc}"], sem=f"wgs{cg}")

        def xload(t, gate):
            S.dma("sp", x1s[:, t, :], xtile_ap(t), reads=[gate], writes=[f"x1_{t}"], sem=f"x1_{t}")

        for cg in range(3):
            for kc in range(8):
                pend.append((lambda cg=cg, kc=kc: load_wgroup(cg, kc)))
            if cg == 0:
                for _ in range(4):
                    if xt_list:
                        pend.append((lambda t=xt_list.pop(0): xload(t, "wg0_7")))
        for kc in range(8):
            pend.append((lambda kc=kc: S.dma("pool", wO[:, kc, :], woe_d[kc * 128:(kc + 1) * 128, :], writes=[f"wO{kc}"], sem="wOs")))
        for t in xt_list:
            pend.append((lambda t=t: xload(t, "wg2_7")))

        def l1w_cast(kc):
            def go():
                S.dma("pool", wnat[kc * 128:(kc + 1) * 128, :], wio_d[kc * 128:(kc + 1) * 128, :], writes=[f"wnat{kc}"], sem="wnatA", max_dma_last_dim=4096)
                S.dma("pool", wonat[kc * 128:(kc + 1) * 128, :], woo_d[kc * 128:(kc + 1) * 128, :], writes=[f"wonat{kc}"], sem="wnatA", max_dma_last_dim=4096)
            return go

        NATK = [f"wnat{kc}" for kc in range(8)]
        ONATK = [f"wonat{kc}" for kc in range(8)]

        def l1w_thunk(idx):
            def go():
                dst = wscr[idx]
                if idx < 8 or 12 <= idx < 20:
                    qc = idx if idx < 8 else idx - 12
                    base = 0 if idx < 8 else 1536
                    c, g = qc // 4, qc % 4
                    for half in range(2):
                        o = base + (8 * c + 4 * half + g) * 64
                        S.dma("sp", dst.rearrange("p (k h d) -> p k h d", h=2, d=64)[:, :, half, :],
                              wnat[:, o:o + 64].rearrange("(k p) d -> p k d", p=128), reads=NATK, writes=[f"wscr{idx}"], sem="wscrB")
                elif idx < 12:
                    o = 1024 + (idx - 8) * 128
                    S.dma("sp", dst.rearrange("p (k c) -> p k c", c=128), wnat[:, o:o + 128].rearrange("(k p) c -> p k c", p=128),
                          reads=NATK, writes=[f"wscr{idx}"], sem="wscrA")
                else:
                    qc = idx - 20
                    c, g = qc // 4, qc % 4
                    for half in range(2):
                        r0 = (8 * c + 4 * half + g) * 64
                        S.dma("sp", dst[half * 64:(half + 1) * 64, :], wonat[r0:r0 + 64, :], reads=ONATK, writes=[f"wscr{idx}"], sem="wscrC")
            return go

        l1_order = [8, 9, 10, 11] + list(range(20, 28)) + list(range(0, 8)) + list(range(12, 20))

        def prefetch(n):
            for _ in range(n):
                if pend:
                    pend.pop(0)()

        prefetch(38)

        hn_rr = [0]

        def norm_stage(tiles):
            nt_ = len(tiles)
            slots = []
            for j, (src, key, ntok, col0) in enumerate(tiles):
                s = hn_rr[0]; hn_rr[0] ^= 1
                slots.append(s)
                S.op("act", (lambda e, src=src, ntok=ntok, j=j, s=s: e.activation(out=hn[0:ntok, s, :], in_=src, func=AF.Square, accum_out=ssq[0:ntok, j:j + 1])),
                     reads=[key], writes=[f"ssq{j}", f"hn{s}"])
            S.op("dve", lambda e: e.tensor_scalar(out=ms[:, 0:nt_], in0=ssq[:, 0:nt_], scalar1=1.0 / 1024, scalar2=RMS_EPS, op0=ALU.mult, op1=ALU.add),
                 reads=[f"ssq{j}" for j in range(nt_)], writes=[f"ms{j}" for j in range(nt_)])
            S.op("pool", lambda e: e.tensor_tensor(out=rstd[:, 0:nt_], in0=ms[:, 0:nt_], in1=nhalf[:, 0:nt_], op=ALU.pow),
                 reads=[f"ms{j}" for j in range(nt_)] + ["nhalf"], writes=[f"rstd{j}" for j in range(nt_)])
            for j, (src, key, ntok, col0) in enumerate(tiles):
                s = slots[j]
                S.op("dve", (lambda e, src=src, ntok=ntok, j=j, s=s: e.scalar_tensor_tensor(out=hn[0:ntok, s, :], in0=src, scalar=rstd[0:ntok, j:j + 1], in1=gt[0:ntok, :], op0=ALU.mult, op1=ALU.mult)),
                     reads=[key, f"rstd{j}", "gt"], writes=[f"hn{s}"])
                b = bank()
                pst = ps[b][:, :].bitcast(BF16).rearrange("p (k t) -> p k t", t=128)
                for kc in range(8):
                    S.op("pe", (lambda e, kc=kc, ntok=ntok, s=s, pst=pst: e.transpose(pst[:, kc, 0:ntok], hn[0:ntok, s, kc * 128:(kc + 1) * 128], id16[0:ntok, 0:ntok])),
                         reads=[f"hn{s}", "id16"], writes=[f"ps{b}"], signal=(kc == 7))
                S.op("act", (lambda e, pst=pst, ntok=ntok, col0=col0: e.activation(out=hT[:, :, col0:col0 + ntok], in_=pst[:, :, 0:ntok], func=AF.Copy)),
                     reads=[f"ps{b}"], writes=[f"hT{j}"])
            return ["hT0", "hT1", "hT2", "hT3"]

        def norm_stats(tiles4):
            junk = xpre[:, 512:1024].bitcast(BF16)
            for j, (src, key, ntok, col0) in enumerate(tiles4):
                S.op("act", (lambda e, src=src, ntok=ntok, j=j: e.activation(out=junk[0:ntok, :], in_=src, func=AF.Square, accum_out=ssq[0:ntok, j:j + 1])),
                     reads=[key], writes=[f"ssq{j}", "zt1"])
            S.op("dve", lambda e: e.tensor_scalar(out=ms[:, 0:4], in0=ssq[:, 0:4], scalar1=1.0 / 1024, scalar2=RMS_EPS, op0=ALU.mult, op1=ALU.add),
                 reads=[f"ssq{j}" for j in range(4)], writes=[f"ms{j}" for j in range(4)])
            S.op("pool", lambda e: e.tensor_tensor(out=rstd[:, 0:4], in0=ms[:, 0:4], in1=nhalf[:, 0:4], op=ALU.pow),
                 reads=[f"ms{j}" for j in range(4)] + ["nhalf"], writes=[f"rstd{j}" for j in range(4)])

        def norm_pre(tile, j):
            src, key, ntok, col0 = tile
            s = hn_rr[0]; hn_rr[0] ^= 1
            S.op("dve", (lambda e: e.scalar_tensor_tensor(out=hn[0:ntok, s, :], in0=src, scalar=rstd[0:ntok, j:j + 1], in1=gt[0:ntok, :], op0=ALU.mult, op1=ALU.mult)),
                 reads=[key, f"rstd{j}", "gt"], writes=[f"hn{s}"])
            return s

        def norm_post(tile, j, s):
            src, key, ntok, col0 = tile
            b = bank()
            pst = ps[b][:, :].bitcast(BF16).rearrange("p (k t) -> p k t", t=128)
            for kc in range(8):
                S.op("pe", (lambda e, kc=kc: e.transpose(pst[:, kc, 0:ntok], hn[0:ntok, s, kc * 128:(kc + 1) * 128], id16[0:ntok, 0:ntok])),
                     reads=[f"hn{s}", "id16"], writes=[f"ps{b}"], signal=(kc == 7))
            S.op("act", (lambda e: e.activation(out=hT[:, :, col0:col0 + ntok], in_=pst[:, :, 0:ntok], func=AF.Copy)),
                 reads=[f"ps{b}"], writes=[f"hT{j}"])

        pf_rate = [2]
        slot2 = {"u": 0, "v": 0, "d": 0, "za": 0, "ct": 0, "zb": 0}

        def nxt(k):
            if k in ("za", "ct", "zb"):
                return 0
            s = slot2[k]
            slot2[k] ^= 1
            return s

        def l0_mt(tiles, N, out_specs, sample=False, first_main=False, prenormed=False, next_tiles=None, delay_bd=False, hooks={}):
            pend_bd = [None]
            if prenormed:
                hkeys = ["hT0", "hT1", "hT2", "hT3"]
            else:
                hkeys = norm_stage(tiles)

            def inproj(fc):
                prefetch(pf_rate[0])
                b = bank()
                mm_group(ps[b][:, 0:N], [(wbig[:, kc, fc * 128:(fc + 1) * 128], hT[:, kc, 0:N]) for kc in range(8)], hkeys + [f"wg{fc // 8}_{kc}" for kc in range(8)], f"ps{b}")
                return b

            for g in range(4):
                w = 2 ** (g + 1)
                bu = inproj(g)
                su = nxt("u")
                U = ubuf[:, su, :]
                if not sample:
                    S.op("pool", (lambda e, U=U, g=g: e.tensor_copy(out=U[:, 0:16], in_=ucar[:, g, :])), reads=[f"ucar{g}"], writes=[f"Uc{su}"])
                S.op("act", (lambda e, U=U, bu=bu: e.activation(out=U[:, 16:16 + N], in_=ps[bu][:, 0:N], func=AF.Copy)), reads=[f"ps{bu}"], writes=[f"Un{su}"])
                bz = inproj(4 + g)
                if delay_bd:
                    sz = g % 2
                    sza_ap = sza[:, 0, 0:N] if sz == 0 else xpre[:, 0:N]
                    szk = [f"sza{sz}"] + (["xpre"] if sz == 1 else [])
                else:
                    sz = 0
                    sza_ap = sza[:, 0, 0:N]
                    szk = ["sza0"]
                S.op("act", (lambda e, sza_ap=sza_ap, bz=bz: e.activation(out=sza_ap, in_=ps[bz][:, 0:N], func=AF.Silu)), reads=[f"ps{bz}"], writes=szk)
                if pend_bd[0] is not None:
                    pend_bd[0]()
                    pend_bd[0] = None
                sd = nxt("d")
                if not sample:
                    S.op("pool", (lambda e, U=U, g=g: e.tensor_copy(out=ucar[:, g, :], in_=U[:, N:N + 16])), reads=[f"Un{su}", f"Uc{su}"], writes=[f"ucar{g}"])
                    cur, curkey = U, None
                    src, dst = None, None
                    bufs = [(tA, "tA"), (tB, "tB")]
                    prev, prevkeys = U, [f"Un{su}", f"Uc{su}"]
                    sh = 1
                    for lvl in range(g + 1):
                        ob, okey = bufs[lvl % 2]
                        lo = 2 * sh - 1
                        S.op("dve" if g >= 2 else "pool", (lambda e, ob=ob, prev=prev, sh=sh, lo=lo: e.tensor_tensor(out=ob[:, lo:16 + N], in0=prev[:, lo:16 + N], in1=prev[:, lo - sh:16 + N - sh], op=ALU.add)),
                             reads=prevkeys, writes=[okey])
                        prev, prevkeys = ob, [okey]
                        sh *= 2
                    if first_main:
                        S.op("pool", (lambda e, prev=prev, g=g: e.tensor_tensor(out=prev[:, 16:32], in0=prev[:, 16:32], in1=rcw[:, g, :], op=ALU.mult)),
                             reads=prevkeys + ["rcw"], writes=prevkeys)
                    S.op("dve", (lambda e, prev=prev, U=U, sd=sd, w=w: e.scalar_tensor_tensor(out=dT[:, sd, 0:N], in0=prev[:, 16:16 + N], scalar=1.0 / w, in1=U[:, 16:16 + N], op0=ALU.mult, op1=ALU.subtract)),
                         reads=prevkeys + [f"Un{su}"], writes=[f"dT{sd}"])
                else:
                    bs = bank()
                    for h in range(2):
                        S.op("pe", (lambda e, h=h, g=g, bs=bs: e.matmul(ps[bs][:, 8 * h:8 * h + 8], lhsT=spl[0:120, h, g * 128:(g + 1) * 128], rhs=selp[0:120, g, :], start=True, stop=True)),
                             reads=["cbf0", "cbf1", "selp"], writes=[f"ps{bs}"], signal=(h == 1))
                    S.op("dve", (lambda e, U=U, bs=bs: e.tensor_tensor(out=tA[:, 0:16], in0=ps[bs][:, 0:16], in1=U[:, 16:32], op=ALU.add)),
                         reads=[f"ps{bs}", f"Un{su}"], writes=["tA"])
                    S.op("dve", (lambda e, U=U, sd=sd, w=w: e.scalar_tensor_tensor(out=dT[:, sd, 0:N], in0=tA[:, 0:16], scalar=1.0 / w, in1=U[:, 16:32], op0=ALU.mult, op1=ALU.subtract)),
                         reads=["tA", f"Un{su}"], writes=[f"dT{sd}"])
                    S.op("pool", (lambda e, U=U, g=g: e.tensor_copy(out=lastT[:, g, 0:16], in_=U[:, 16:32])), reads=[f"Un{su}"], writes=["lastT"])
                def blockdiag(g=g, sd=sd, sz=sz, sza_ap=sza_ap):
                    bb = bank()
                    mm_group(ps[bb][:, 0:N], [(poolw[:, g, :], dT[:, sd, 0:N])], [f"dT{sd}", "poolw"], f"ps{bb}")
                    S.op("dve", (lambda e: e.scalar_tensor_tensor(out=ycat[:, g, 0:N], in0=ps[bb][:, 0:N], scalar=gsm[:, g:g + 1], in1=sza_ap, op0=ALU.mult, op1=ALU.mult)),
                         reads=[f"ps{bb}", f"sza{sz}", "gsm"], writes=[f"ycat{g}"])

                if delay_bd:
                    pend_bd[0] = blockdiag
                else:
                    blockdiag()

            for fn in hooks.get("mid", []):
                fn()
            nslots = {}
            if next_tiles is not None:
                norm_stats(next_tiles)
            for i in range(4):
                if i == 2 and next_tiles is not None:
                    nslots[0] = norm_pre(next_tiles[0], 0)
                    nslots[1] = norm_pre(next_tiles[1], 1)
                bc = inproj(12 + i)
                sc = nxt("ct")
                S.op("act", (lambda e, sc=sc, bc=bc: e.activation(out=ctmp[:, sc, 0:N], in_=ps[bc][:, 0:N], func=AF.Copy)), reads=[f"ps{bc}"], writes=[f"ct{sc}"])
                bv = inproj(16 + i)
                if pend_bd[0] is not None:
                    pend_bd[0]()
                    pend_bd[0] = None
                sv = nxt("v")
                V = vbuf[:, sv, :]
                if not sample:
                    S.op("pool", (lambda e, V=V, i=i: e.tensor_copy(out=V[:, 0:2], in_=vcar[:, i, :])), reads=[f"vcar{i}"], writes=[f"Vc{sv}"])
                S.op("dve", (lambda e, V=V, bv=bv, sc=sc: e.tensor_tensor(out=V[:, 2:2 + N], in0=ps[bv][:, 0:N], in1=ctmp[:, sc, 0:N], op=ALU.mult)),
                     reads=[f"ps{bv}", f"ct{sc}"], writes=[f"Vn{sv}"])
                w0 = gsm[:, 4 + i:5 + i]; w1 = gsm[:, 8 + i:9 + i]; w2 = gsm[:, 12 + i:13 + i]
                if not sample:
                    S.op("pool", (lambda e, V=V, i=i: e.tensor_copy(out=vcar[:, i, :], in_=V[:, N:N + 2])), reads=[f"Vn{sv}", f"Vc{sv}"], writes=[f"vcar{i}"])
                    vk = [f"Vn{sv}", f"Vc{sv}"]
                    S.op("dve", (lambda e, V=V, w0=w0: e.tensor_scalar(out=acc[:, 0, 0:N], in0=V[:, 0:N], scalar1=w0, scalar2=None, op0=ALU.mult)), reads=vk + ["gsm"], writes=["acc"])
                    S.op("dve", (lambda e, V=V, w1=w1: e.scalar_tensor_tensor(out=acc[:, 0, 0:N], in0=V[:, 1:1 + N], scalar=w1, in1=acc[:, 0, 0:N], op0=ALU.mult, op1=ALU.add)), reads=vk + ["acc"], writes=["acc"])
                else:
                    bs = bank()
                    for r in range(2):
                        S.op("pe", (lambda e, r=r, i=i, bs=bs: e.matmul(ps[bs][:, 16 * r:16 * r + 16], lhsT=scv[0:32, i * 128:(i + 1) * 128], rhs=selc[0:32, r, :], start=True, stop=True)),
                             reads=["scv", "selc"], writes=[f"ps{bs}"], signal=(r == 1))
                    S.op("dve", (lambda e, bs=bs, w0=w0: e.tensor_scalar(out=acc[:, 0, 0:N], in0=ps[bs][:, 0:16], scalar1=w0, scalar2=None, op0=ALU.mult)), reads=[f"ps{bs}", "gsm"], writes=["acc"])
                    S.op("dve", (lambda e, bs=bs, w1=w1: e.scalar_tensor_tensor(out=acc[:, 0, 0:N], in0=ps[bs][:, 16:32], scalar=w1, in1=acc[:, 0, 0:N], op0=ALU.mult, op1=ALU.add)), reads=[f"ps{bs}", "acc"], writes=["acc"])
                    S.op("pool", (lambda e, V=V, i=i: e.tensor_copy(out=lastT[:, i, 16:32], in_=V[:, 2:18])), reads=[f"Vn{sv}"], writes=["lastT"])
                S.op("dve", (lambda e, V=V, w2=w2: e.scalar_tensor_tensor(out=acc[:, 0, 0:N], in0=V[:, 2:2 + N], scalar=w2, in1=acc[:, 0, 0:N], op0=ALU.mult, op1=ALU.add)), reads=[f"Vn{sv}", "acc"], writes=["acc"])
                bbg = inproj(8 + i)
                S.op("dve", (lambda e, bbg=bbg: e.tensor_tensor(out=acc[:, 0, 0:N], in0=ps[bbg][:, 0:N], in1=acc[:, 0, 0:N], op=ALU.mult)), reads=[f"ps{bbg}", "acc"], writes=["acc"])
                bzb = inproj(20 + i)
                szs = nxt("zb")
                S.op("act", (lambda e, szs=szs, bzb=bzb: e.activation(out=szb[:, szs, 0:N], in_=ps[bzb][:, 0:N], func=AF.Silu)), reads=[f"ps{bzb}"], writes=[f"szb{szs}"])
                S.op("dve", (lambda e, szs=szs, i=i: e.tensor_tensor(out=ycat[:, 4 + i, 0:N], in0=acc[:, 0, 0:N], in1=szb[:, szs, 0:N], op=ALU.mult)), reads=["acc", f"szb{szs}"], writes=[f"ycat{4 + i}"])

            for fn in hooks.get("late", []):
                fn()
            ykeys = [f"ycat{c}" for c in range(8)]
            for oi, (xap, xkey, ntok, col0) in enumerate(out_specs):
                if next_tiles is not None and oi == 0:
                    norm_post(next_tiles[0], 0, nslots[0])
                    norm_post(next_tiles[1], 1, nslots[1])
                    nslots[2] = norm_pre(next_tiles[2], 2)
                    nslots[3] = norm_pre(next_tiles[3], 3)
                if next_tiles is not None and oi == 2:
                    norm_post(next_tiles[2], 2, nslots[2])
                    norm_post(next_tiles[3], 3, nslots[3])
                bs2 = []
                for n2 in range(2):
                    b = bank(hold=True)
                    bs2.append(b)
                    mm_group(ps[b][0:ntok, :], [(ycat[:, kc, col0:col0 + ntok], wO[:, kc, n2 * 512:(n2 + 1) * 512]) for kc in range(8)],
                             ykeys + [f"wO{kc}" for kc in range(8)], f"ps{b}")
                for n2, b in enumerate(bs2):
                    S.op("dve", (lambda e, xap=xap, b=b, ntok=ntok, n2=n2: e.tensor_tensor(out=xap[:, n2 * 512:(n2 + 1) * 512], in0=ps[b][0:ntok, :], in1=xap[:, n2 * 512:(n2 + 1) * 512], op=ALU.add)),
                         reads=[f"ps{b}", xkey], writes=[xkey])
                    release(b)

        l0_mt([(xpre[0:16, :], "xpre", 16, 0), (x1s[:, 0, :], "x1_0", 128, 16)], 144,
              [(x1s[:, 0, :], "x1_0", 128, 16)])
        l0_mt([(xsm[0:16, :], "xsm", 16, 0)], 16, [(xsm[0:16, :], "xsm", 16, 0)], sample=True)
        bt = bank()
        for c in range(4):
            S.op("pe", (lambda e, c=c, bt=bt: e.transpose(ps[bt][0:32, c * 128:(c + 1) * 128], lastT[:, c, :], idf[:, :])), reads=["lastT", "idf"], writes=[f"ps{bt}"], signal=(c == 3))
        S.op("act", lambda e, bt=bt: e.activation(out=tA[0:32, 0:512], in_=ps[bt][0:32, :], func=AF.Copy), reads=[f"ps{bt}"], writes=["tA"])
        S.dma("pool", nps_d[:, 14, :], tA[0:16, 0:512], reads=["tA"], sem="outs")
        S.dma("pool", ncs_d[:, 1, :], tA[16:32, 0:512], reads=["tA"], sem="outs")
        S.dma("sp", nps_d[:, 0:14, :], spool_d.rearrange("(s r) f -> s r f", r=15)[:, 1:15, :], sem="outs2")
        S.dma("sp", ncs_d[:, 0, :], sconv_d.rearrange("(s r) f -> s r f", r=2)[:, 1, :], sem="outs2")
        TWO_PI = 2.0 * math.pi
        C1 = 6.28125
        C2 = TWO_PI - C1
        LIM = 3.1415925
        W = NS + 16

        def table_chunk_stages(c0, n, sample_cols, sl):
            A = stg[:, sl, 0:n]; B = stg[:, sl, 512:512 + n]
            ka, kb_ = f"tbA{sl}", f"tbB{sl}"
            ti = tmpi[:, 0:n]

            def st1():
                if sample_cols:
                    S.op("dve", lambda e: e.memset(A, float(PAST_LEN)), writes=[ka])
                    S.op("dve", lambda e: e.tensor_scalar(out=A, in0=A, scalar1=invf[:, 0:1], scalar2=None, op0=ALU.mult), reads=[ka, "invf"], writes=[ka])
                else:
                    S.op("pool", lambda e: e.iota(ti, pattern=[[1, n]], base=c0 - 144, channel_multiplier=0), writes=["tmpi", "tmpi2", "tmpi3", "tmpi4", "tmpi5"])
                    S.op("dve", lambda e: e.tensor_scalar(out=A, in0=ti, scalar1=metab[:, 0:1], scalar2=invf[:, 0:1], op0=ALU.add, op1=ALU.mult),
                         reads=["tmpi", "metab", "invf"], writes=[ka])
                S.op("dve", lambda e: e.tensor_scalar(out=B, in0=A, scalar1=1.0 / TWO_PI, scalar2=None, op0=ALU.mult), reads=[ka], writes=[kb_])
                S.op("dve", lambda e: e.tensor_copy(out=ti, in_=B), reads=[kb_], writes=["tmpi"])
                S.op("dve", lambda e: e.tensor_copy(out=B, in_=ti), reads=["tmpi"], writes=[kb_])
                S.op("dve", lambda e: e.scalar_tensor_tensor(out=A, in0=B, scalar=-C1, in1=A, op0=ALU.mult, op1=ALU.add), reads=[ka, kb_], writes=[ka])
                S.op("dve", lambda e: e.scalar_tensor_tensor(out=A, in0=B, scalar=-C2, in1=A, op0=ALU.mult, op1=ALU.add), reads=[ka, kb_], writes=[ka])
                S.op("dve", lambda e: e.tensor_scalar(out=A, in0=A, scalar1=LIM, scalar2=-LIM, op0=ALU.min, op1=ALU.max), reads=[ka], writes=[ka])

            def st2():
                S.op("act", lambda e: e.activation(out=B, in_=A, func=AF.Sin), reads=[ka], writes=[kb_])
                S.op("dve", lambda e: e.scalar_tensor_tensor(out=A, in0=A, scalar=-1.0, in1=A, op0=ALU.mult, op1=ALU.max), reads=[ka, kb_], writes=[ka])
                S.op("dve", lambda e: e.tensor_scalar(out=A, in0=A, scalar1=-1.0, scalar2=math.pi / 2, op0=ALU.mult, op1=ALU.add), reads=[ka], writes=[ka])

            def st3():
                S.op("dve", lambda e: e.tensor_scalar(out=B, in0=B, scalar1=sgn[:, 0:1], scalar2=None, op0=ALU.mult), reads=[kb_, "sgn"], writes=[kb_])
                S.dma("act", tabs[1][:, c0:c0 + n], B, reads=[kb_], writes=["tabs"], sem="tabs")
                S.op("act", lambda e: e.activation(out=A, in_=A, func=AF.Sin), reads=[ka], writes=[ka])
                S.dma("act", tabs[0][:, c0:c0 + n], A, reads=[ka], writes=["tabs"], sem="tabs")

            return st1, st2, st3

        tab_chunks = [(c0, min(512, NS - c0), False) for c0 in range(0, NS, 512)] + [(NS, 16, True)]

        for m in range(NMT):
            if m == min(1, NMT - 1):
                for kc in range(8):
                    pend.append(l1w_cast(kc))
                for idx in l1_order:
                    pend.append(l1w_thunk(idx))
            tl = [(x1s[:, 1 + 4 * m + j, :], f"x1_{1 + 4 * m + j}", 128, 128 * j) for j in range(4)]
            ntl0 = [(x1s[:, 5 + 4 * m + j, :], f"x1_{5 + 4 * m + j}", 128, 128 * j) for j in range(4)] if m + 1 < NMT else None
            per = min(2, (len(tab_chunks) + NMT - 1 - m) // (NMT - m))
            hooks = {"mid": [], "late": []}
            for sl in range(per):
                if tab_chunks:
                    st1, st2, st3 = table_chunk_stages(*tab_chunks.pop(0), sl)
                    st1()
                    hooks["mid"].append(st2)
                    hooks["late"].append(st3)
            l0_mt(tl, 512, tl, first_main=(m == 0), prenormed=(m > 0), next_tiles=ntl0, delay_bd=True, hooks=hooks)
        for c in range(4):
            S.op("pool", (lambda e, c=c: e.tensor_copy(out=lastT[:, c, 0:16], in_=ucar[:, c, :])), reads=[f"ucar{c}", "lastT"], writes=["lastT"])
            S.op("pool", (lambda e, c=c: e.tensor_copy(out=lastT[:, c, 16:18], in_=vcar[:, c, :])), reads=[f"vcar{c}", "lastT"], writes=["lastT"])
        bt = bank()
        for c in range(4):
            S.op("pe", (lambda e, c=c, bt=bt: e.transpose(ps[bt][0:32, c * 128:(c + 1) * 128], lastT[:, c, :], idf[:, :])), reads=["lastT", "idf"], writes=[f"ps{bt}"], signal=(c == 3))
        S.op("act", lambda e, bt=bt: e.activation(out=tA[0:32, 0:512], in_=ps[bt][0:32, :], func=AF.Copy), reads=[f"ps{bt}"], writes=["tA"])
        S.dma("pool", npp_d[:, :], tA[1:16, 0:512], reads=["tA"], sem="outs")
        S.dma("pool", ncp_d[:, :], tA[16:18, 0:512], reads=["tA"], sem="outs")

        def finish_debug():
            S.wait_all("sp", [k for k in S.sems if k.startswith("outs") or k.startswith("wscr")])
            S.wait_all("pool", ["outs"])
            S.barrier()
            S.replay()

        if stop_after_l0:
            for t in range(1, NT + 1):
                S.dma("sp", y_d[128 * (t - 1):128 * t, :], x1s[:, t, :], reads=[f"x1_{t}"], sem="outs")
            S.dma("sp", ys_d[:, :], xsm[0:16, :], reads=["xsm"], sem="outs")
            S.wait_all("sp", ["outs"])
            S.replay()
            return nc

        while tab_chunks:
            for st in table_chunk_stages(*tab_chunks.pop(0), 0):
                st()
        prefetch(1000)
        S.barrier()
        W = NS + 16
        C = Carver()
        cosT = C.alloc([W])[:, :]; sinT = C.alloc([W])[:, :]
        wr = C.alloc([6, 1024], BF16)
        qT = C.alloc([8, 512], BF16); kf32 = C.alloc([2, 128])
        kTp = stg[:, :, :].rearrange("p a b -> p (a b)")[:, 0:1280].bitcast(BF16).rearrange("p (c h x) -> p c h x", c=2, h=2)
        onesp = C.alloc([2, 128], BF16)
        zs = C.alloc([8, 512], BF16)
        vTp = C.alloc([5 * 4, 128], BF16).rearrange("p (s h) x -> p s h x", h=4)
        PT = C.alloc([2, 512], BF16)
        _o = C.off
        masks = C.alloc([4, 512], BF16)
        sinkexp = C.alloc([2, 512])
        _e = C.off
        C.off = _o
        ckb = C.alloc([4, 256], BF16); vsg = C.alloc([4, 256], BF16); ktg = C.alloc([8, 128], BF16)
        assert C.off <= _e
        C.off = _e
        q32 = C.alloc([2, 512]); qsq = C.alloc([2, 512], BF16); rs = C.alloc([1, 512]); qn = C.alloc([2, 512], BF16)
        t1 = C.alloc([1, 512]); t2 = C.alloc([1, 512])
        og = C.alloc([2, 1024], BF16)
        rr = C.alloc([1, 512]); ogt = C.alloc([1, 512], BF16)
        zt = C.alloc([1, 512])
        pts = C.alloc([64], BF16)[:, :]; ossm = C.alloc([32])[:, :]; rsm = C.alloc([32])[:, :]; sk16 = C.alloc([32])[:, :]
        vsb = C.alloc([256], BF16)[:, :]

        S.dma("sp", gt[:], ng_d[1:2, :].partition_broadcast(128), writes=["gt"], sem="c2")
        S.op("pool", lambda e: e.memset(kTp, 0.0), writes=["kT0", "kT1", "stg0", "stg1", "tbA0", "tbB0", "tbA1", "tbB1"])
        S.op("pool", lambda e: e.memset(vTp, 0.0), writes=[f"vT{i}" for i in range(5)])
        S.op("pool", lambda e: e.memset(onesp, 0.0), writes=["onesp"])
        S.op("pool", lambda e: e.memset(onesp[:, 0, 0:64], 1.0), reads=["onesp"], writes=["onesp"])
        S.op("pool", lambda e: e.memset(onesp[:, 1, 64:128], 1.0), reads=["onesp"], writes=["onesp"])
        for ch in range(8):
            S.dma("sp", wO[:, ch, :], wscr[20 + ch], reads=[f"wscr{20 + ch}"], writes=[f"wO{ch}"], sem="c2")

        ring_plan = [10, 11, 8, 9]
        _full = [10, 11, 8, 9]
        for i_ in range(8):
            _full += [i_, 12 + i_]
        for m in range(NMT + 1):
            ring_plan += _full
        ring_pos = [0]
        ring_use = [0]

        def ring_issue():
            i = ring_pos[0]
            if i >= len(ring_plan):
                return
            ring_pos[0] += 1
            sl = i % 6
            S.dma("sp", wr[:, sl, :], wscr[ring_plan[i]], reads=[f"wscr{ring_plan[i]}"], writes=[f"wr{sl}"], sem=f"wr{sl}")

        def ring_take(expect):
            i = ring_use[0]
            assert ring_plan[i] == expect, (i, ring_plan[i], expect)
            ring_use[0] += 1
            while ring_pos[0] < min(i + 5, len(ring_plan)):
                ring_issue()
            sl = i % 6
            return wr[:, sl, :].rearrange("p (k c) -> p k c", c=128), f"wr{sl}"

        for _ in range(4):
            ring_issue()

        S.op("pool", lambda e: e.iota(tmpi[:, :], pattern=[[0, 4], [1, 128]], base=0, channel_multiplier=-1), writes=["tmpi", "tmpi2", "tmpi3", "tmpi4", "tmpi5"])
        S.op("dve", lambda e: e.tensor_single_scalar(out=masks[:, 0, :], in_=tmpi[:, :], scalar=0, op=ALU.is_lt), reads=["tmpi"], writes=["mask0"])
        S.op("dve", lambda e: e.tensor_single_scalar(out=masks[:, 1, :], in_=tmpi[:, :], scalar=0, op=ALU.is_ge), reads=["tmpi"], writes=["mask1"])
        S.op("dve", lambda e: e.tensor_scalar(out=masks[:, 2, :], in0=masks[:, 0, :], scalar1=metab[:, 1:2], scalar2=None, op0=ALU.mult), reads=["mask0", "metab"], writes=["mask2"])
        S.op("dve", lambda e: e.tensor_single_scalar(out=masks[:, 3, :], in_=tmpi[:, :], scalar=0, op=ALU.is_ge), reads=["tmpi"], writes=["mask2"])
        S.op("act", lambda e: e.activation(out=ske[:], in_=skb[:], func=AF.Exp), reads=["skb"], writes=["ske"])
        for half in range(2):
            for c in range(2):
                hh = (2 * c + half) * 4
                S.op("pool", (lambda e, half=half, c=c, hh=hh: e.tensor_copy(
                    out=sinkexp[half * 64:(half + 1) * 64, c, :].rearrange("p (g q) -> p g q", q=128),
                    in_=ske[half * 64:(half + 1) * 64, hh:hh + 4].unsqueeze(2).to_broadcast([64, 4, 128]))),
                    reads=["ske"], writes=["sinkexp"])
                S.op("pool", (lambda e, half=half, c=c, hh=hh: e.tensor_copy(
                    out=sk16[half * 64:(half + 1) * 64, :].rearrange("p (b c g) -> p b c g", c=2, g=4)[:, :, c, :],
                    in_=ske[half * 64:(half + 1) * 64, hh:hh + 4].unsqueeze(1).to_broadcast([64, 4, 4]))),
                    reads=["ske"], writes=["sk16"])
        S.dma("sp", cosT[:, :], tabs[0], reads=["tabs"], writes=["cosT"], sem="c2")
        S.dma("sp", sinT[:, :], tabs[1], reads=["tabs"], writes=["sinT"], sem="c2")
        if stop_at == 'tables':
            finish_debug()
            return nc
        slot1 = {"q": 0, "pt": 0, "og": 0}

        def nx1(k):
            v = slot1[k]
            slot1[k] ^= 1
            return v

        def qk_pipeline(descs, N, tcol0, zlist=()):
            HK = ["hT0", "hT1", "hT2", "hT3"]
            st = {}

            def stage_a(n):
                chunk_idx = descs[n][0]
                wv, wkey = ring_take(chunk_idx)
                b = bank()
                mm_group(ps[b][:, 0:N], [(wv[:, kc, :], hT[:, kc, 0:N]) for kc in range(8)], HK + [wkey], f"ps{b}")
                sq = nx1("q")
                S.op("dve", (lambda e, b=b, sq=sq: e.tensor_copy(out=q32[:, sq, 0:N], in_=ps[b][:, 0:N])), reads=[f"ps{b}"], writes=[f"q32_{sq}"])
                S.op("act", (lambda e, sq=sq: e.activation(out=qsq[:, sq, 0:N], in_=q32[:, sq, 0:N], func=AF.Square)), reads=[f"q32_{sq}"], writes=[f"qsq{sq}"])
                st[n] = sq

            def stage_b(n):
                sq = st[n]
                gcol = descs[n][3]
                sn = n % 2
                bs = bank()
                mm_group(ps[bs][:, 0:N], [(blk64[:, :], qsq[:, sq, 0:N])], [f"qsq{sq}", "blk64"], f"ps{bs}")
                S.op("act", (lambda e, bs=bs: e.activation(out=rs[:, 0, 0:N], in_=ps[bs][:, 0:N], func=AF.Ln, scale=1.0 / 64, bias=epsb[:, 0:1])), reads=[f"ps{bs}", "epsb"], writes=["rs"])
                S.op("act", (lambda e: e.activation(out=rs[:, 0, 0:N], in_=rs[:, 0, 0:N], func=AF.Exp, scale=-0.5)), reads=["rs"], writes=["rs"])
                S.op("dve", (lambda e, sq=sq, sn=sn, gcol=gcol: e.scalar_tensor_tensor(out=qn[:, sn, 0:N], in0=q32[:, sq, 0:N], scalar=gq2[:, gcol:gcol + 1], in1=rs[:, 0, 0:N], op0=ALU.mult, op1=ALU.mult)),
                     reads=[f"q32_{sq}", "rs", "gq2"], writes=[f"qn{sn}"])

            def stage_c(n):
                _, out_ap, out_key, gcol, f32_out = descs[n]
                sn = n % 2
                br = bank()
                mm_group(ps[br][:, 0:N], [(perm16[:, :], qn[:, sn, 0:N])], [f"qn{sn}", "perm16"], f"ps{br}")
                S.op("pool", (lambda e, sn=sn: e.tensor_tensor(out=t1[:, 0, 0:N], in0=qn[:, sn, 0:N], in1=cosT[:, tcol0:tcol0 + N], op=ALU.mult)), reads=[f"qn{sn}", "cosT"], writes=["t1"])
                S.op("dve", (lambda e, br=br: e.tensor_tensor(out=t2[:, 0, 0:N], in0=ps[br][:, 0:N], in1=sinT[:, tcol0:tcol0 + N], op=ALU.mult)), reads=[f"ps{br}", "sinT"], writes=["t2"])
                for (lo_, hi_, oap) in out_ap:
                    S.op("pool", (lambda e, oap=oap, lo_=lo_, hi_=hi_: e.tensor_tensor(out=oap, in0=t1[lo_:hi_, 0, 0:N], in1=t2[lo_:hi_, 0, 0:N], op=ALU.add)), reads=["t1", "t2"], writes=[out_key])
                if f32_out is not None:
                    fo, fkey, lo = f32_out
                    S.op("pool", (lambda e, fo=fo, lo=lo: e.tensor_tensor(out=fo, in0=t1[:, 0, lo:N], in1=t2[:, 0, lo:N], op=ALU.add)), reads=["t1", "t2"], writes=[fkey])

            nd = len(descs)
            zl = list(zlist)
            for n in range(nd + 2):
                if n < nd:
                    stage_a(n)
                if 0 <= n - 1 < nd:
                    stage_b(n - 1)
                if 0 <= n - 2 < nd:
                    stage_c(n - 2)
                if n < nd and descs[n][0] < 8 and zl:
                    z_chunk(zl.pop(0), N)
            for zc in zl:
                z_chunk(zc, N)
            z_flush()

        def v_chunks(tiles, vslots, f32_last=False):
            wv0, k0 = ring_take(10)
            wv1, k1 = ring_take(11)
            for ti_, ((src, key, ntok, col0), vs_) in enumerate(zip(tiles, vslots)):
                b = bank()
                for vc, (wv, wk) in enumerate(((wv0, k0), (wv1, k1))):
                    mm_group(ps[b][0:ntok, vc * 128:(vc + 1) * 128], [(hT[:, kc, col0:col0 + ntok], wv[:, kc, :]) for kc in range(8)],
                             ["hT0", "hT1", "hT2", "hT3", wk], f"ps{b}")
                for half in range(2):
                    S.op("act", (lambda e, b=b, ntok=ntok, vs_=vs_, half=half: e.activation(
                        out=vTp[0:ntok, vs_, :, :].rearrange("p (c hh) x -> p c hh x", hh=2)[:, :, half, half * 64:(half + 1) * 64],
                        in_=ps[b][0:ntok, 0:256].rearrange("p (c hh d) -> p c hh d", hh=2, d=64)[:, :, half, :], func=AF.Copy)),
                        reads=[f"ps{b}"], writes=[f"vT{vs_}"])
                if f32_last and ti_ == len(tiles) - 1:
                    S.op("act", (lambda e, b=b, ntok=ntok: e.activation(out=vf32[0:ntok, :], in_=ps[b][0:ntok, 0:256], func=AF.Copy)), reads=[f"ps{b}"], writes=["vf32"])

        zpend = [None]

        def z_flush():
            if zpend[0] is not None:
                zpend[0]()
                zpend[0] = None

        def z_chunk(zc, N):
            wv, wkey = ring_take(12 + zc)
            b = bank(hold=True)
            zsl = zc % 2
            ztv = zt[:, 0, 0:N] if zsl == 0 else xpre[:, 512:512 + N]
            zk = f"zt{zsl}"
            mm_group(ps[b][:, 0:N], [(wv[:, kc, :], hT[:, kc, 0:N]) for kc in range(8)], ["hT0", "hT1", "hT2", "hT3", wkey], f"ps{b}")
            S.op("act", (lambda e: e.activation(out=ztv, in_=ps[b][:, 0:N], func=AF.Exp, scale=-1.0)), reads=[f"ps{b}"], writes=[zk])
            S.op("act", (lambda e: e.activation(out=ztv, in_=ztv, func=AF.Ln, bias=oneb[:, 0:1])), reads=[zk, "oneb"], writes=[zk])
            S.op("act", (lambda e: e.activation(out=ztv, in_=ztv, func=AF.Exp, scale=-1.0)), reads=[zk], writes=[zk])
            z_flush()

            def fin():
                S.op("dve", (lambda e: e.tensor_tensor(out=zs[:, zc, 0:N], in0=ps[b][:, 0:N], in1=ztv, op=ALU.mult)), reads=[f"ps{b}", zk], writes=[f"zs{zc}"])
                release(b)

            zpend[0] = fin

        qkeys = [f"qT{i}" for i in range(8)]

        def attention_stream(m, tl, ntl):
            LA = 1
            units = [(j, c, half) for j in range(4) for c in range(2) for half in range(2)]
            acc_banks = {}
            deferred = []

            PTpairs = [PT, xpre[:, :].bitcast(BF16)[:, 0:1024].rearrange("p (a b) -> p a b", b=512)]
            ucount = [0]

            def scores(j, c, half):
                qc0 = 128 * j
                pair = bank_pair()
                for kb in range(2):
                    b = pair + kb
                    kc0 = qc0 + 128 * kb
                    S.op("pe", (lambda e, b=b, c=c, half=half, kc0=kc0, qc0=qc0: e.matmul(
                        ps[b].rearrange("p (g q) -> p g q", q=128),
                        lhsT=kTp[:, c, half, kc0:kc0 + 128],
                        rhs=qT[:, 4 * c:4 * c + 4, qc0:qc0 + 128], start=True, stop=True)),
                        reads=[f"kT{c}"] + qkeys, writes=[f"ps{b}"], signal=(kb == 1))
                return pair

            def estage(j, c, half, pair):
                first = (m == 0 and j == 0)
                pp = ucount[0] % 2
                ucount[0] += 1
                PTp = PTpairs[pp]
                pk = f"PTp{pp}"
                mk0 = 2 if first else 0
                S.op("act", (lambda e, pair=pair, PTp=PTp: e.activation(out=PTp.rearrange("p a b -> p (a b)"), in_=psbig[:, pair * 512:(pair + 2) * 512], func=AF.Exp, scale=0.125)),
                     reads=[f"ps{pair}", f"ps{pair + 1}"], writes=[pk])
                release(pair)
                release(pair + 1)
                S.op("dve", (lambda e, PTp=PTp, mk0=mk0: e.tensor_tensor(out=PTp, in0=PTp, in1=masks[:, mk0:mk0 + 2, :], op=ALU.mult)),
                     reads=[pk, "mask0", "mask1", "mask2"], writes=[pk])
                return PTp, pk

            def pstage(j, c, half, PTp, pk):
                h = 2 * c + half
                if half == 0:
                    ab = bank_pair()
                    acc_banks[(j, c)] = (ab, ab + 1)
                bo, bd = acc_banks[(j, c)]
                for kb in range(2):
                    vslot = j + kb
                    S.op("pe", (lambda e, bo=bo, h=h, vslot=vslot, PTp=PTp, kb=kb, half=half: e.matmul(
                        ps[bo], lhsT=vTp[:, vslot, h, :], rhs=PTp[:, kb, :], start=(kb == 0 and half == 0), stop=(kb == 1 and half == 1))),
                        reads=[f"vT{vslot}", pk], writes=[f"ps{bo}"], signal=False)
                for kb in range(2):
                    S.op("pe", (lambda e, bd=bd, PTp=PTp, kb=kb, half=half: e.matmul(
                        ps[bd], lhsT=onesp[:, half, :], rhs=PTp[:, kb, :], start=(kb == 0 and half == 0), stop=(kb == 1 and half == 1))),
                        reads=["onesp", pk], writes=[f"ps{bd}"], signal=(kb == 1))

            def norm_a(j, c):
                bo, bd = acc_banks[(j, c)]
                S.op("dve", (lambda e, bd=bd, c=c: e.tensor_tensor(out=rr[:, 0, :], in0=ps[bd][:, :], in1=sinkexp[:, c, :], op=ALU.add)), reads=[f"ps{bd}", "sinkexp"], writes=["rr"])

            def norm_a2(j, c):
                S.op("act", (lambda e: e.activation(out=rr[:, 0, :], in_=rr[:, 0, :], func=AF.Ln)), reads=["rr"], writes=["rr"])
                S.op("act", (lambda e: e.activation(out=rr[:, 0, :], in_=rr[:, 0, :], func=AF.Exp, scale=-1.0)), reads=["rr"], writes=["rr"])

            def norm_b(j, c):
                bo, bd = acc_banks[(j, c)]
                so = j % 2
                qc0 = 128 * j
                S.op("dve", (lambda e, bo=bo: e.tensor_tensor(out=ogt[:, 0, :], in0=ps[bo][:, :], in1=rr[:, 0, :], op=ALU.mult)), reads=[f"ps{bo}", "rr"], writes=["ogt"])
                S.op("pool", (lambda e, c=c, so=so, qc0=qc0: e.tensor_tensor(
                    out=og[:, so, :].rearrange("p (k q) -> p k q", q=128)[:, 4 * c:4 * c + 4, :],
                    in0=ogt[:, 0, :].rearrange("p (g q) -> p g q", q=128),
                    in1=zs[:, 4 * c:4 * c + 4, qc0:qc0 + 128], op=ALU.mult)),
                    reads=["ogt"] + [f"zs{4 * c + g}" for g in range(4)], writes=[f"og{so}"])
                release(bo)
                release(bd)

            def finish_tile(j, hs_):
                xap, xkey = tl[j][0], tl[j][1]
                if ntl is not None:
                    norm_post(ntl[j], j, hs_)
                out_proj(j % 2, 128, xap, xkey)
                t = 1 + 4 * m + j
                if j % 2 == 1:
                    S.dma("pool", y_d[128 * (t - 2):128 * t, :].rearrange("(a p) f -> p a f", p=128), x1s[:, t - 1:t + 1, :],
                          reads=[tl[j - 1][1], xkey], sem="outs")

            nu = len(units)
            pend_s = {}
            pend_e = {}
            hs_of = {}
            if ntl is not None:
                norm_stats(ntl)
            pend_s[0] = scores(*units[0])
            pend_s[1] = scores(*units[1])
            pend_e[0] = estage(*units[0], pend_s.pop(0))
            for i, (j, c, half) in enumerate(units):
                if c == 0 and half == 0 and ntl is not None:
                    hs_of[j] = norm_pre(ntl[j], j)
                if i + 2 < nu:
                    pend_s[i + 2] = scores(*units[i + 2])
                if i + 1 < nu:
                    pend_e[i + 1] = estage(*units[i + 1], pend_s.pop(i + 1))
                pstage(j, c, half, *pend_e.pop(i))
                for d in sorted([d for d in deferred if d[0] <= i], key=lambda d: d[0]):
                    deferred.remove(d)
                    d[1]()
                if half == 1:
                    deferred.append((i + 1, (lambda j=j, c=c: norm_a(j, c))))
                    deferred.append((i + 2, (lambda j=j, c=c: norm_a2(j, c))))
                    deferred.append((i + 3, (lambda j=j, c=c: norm_b(j, c))))
                    if c == 1:
                        deferred.append((i + 4, (lambda j=j: finish_tile(j, hs_of.get(j)))))
            for d in sorted(deferred, key=lambda d: d[0]):
                d[1]()

        def out_proj(so, ntok, xap, xkey):
            ogv = og[:, so, :].rearrange("p (k q) -> p k q", q=128)
            bs2 = []
            for n2 in range(2):
                b = bank(hold=True)
                bs2.append(b)
                mm_group(ps[b][0:ntok, :], [(ogv[:, ch, 0:ntok], wO[:, ch, n2 * 512:(n2 + 1) * 512]) for ch in range(8)],
                         [f"og{so}"] + [f"wO{ch}" for ch in range(8)], f"ps{b}")
            for n2, b in enumerate(bs2):
                S.op("dve", (lambda e, b=b, n2=n2: e.tensor_tensor(out=xap[:, n2 * 512:(n2 + 1) * 512], in0=ps[b][0:ntok, :], in1=xap[:, n2 * 512:(n2 + 1) * 512], op=ALU.add)),
                     reads=[f"ps{b}", xkey], writes=[xkey])
                release(b)

        norm_stage([(x1s[:, 0, :], "x1_0", 128, 0)])
        v_chunks([(x1s[:, 0, :], "x1_0", 128, 0)], [0])
        qk_pipeline([(8 + kc_, [(0, 64, kTp[0:64, kc_, 0, 0:128]), (64, 128, kTp[64:128, kc_, 1, 0:128])], f"kT{kc_}", 1, None) for kc_ in range(2)], 128, 16)

        if stop_at == 'halo':
            finish_debug()
            return nc
        def main_tiles(m):
            return [(x1s[:, 1 + 4 * m + j, :], f"x1_{1 + 4 * m + j}", 128, 128 * j) for j in range(4)]

        for m in range(NMT):
            tl = main_tiles(m)
            last = (m == NMT - 1)
            if m == 0:
                norm_stage(tl)
            tcol = 144 + 512 * m
            v_chunks(tl, [1, 2, 3, 4], f32_last=last)
            descs = [(8 + kc_, [(0, 64, kTp[0:64, kc_, 0, 128:640]), (64, 128, kTp[64:128, kc_, 1, 128:640])], f"kT{kc_}", 1, ((kf32[:, kc_, :], "kf32", 384) if last else None)) for kc_ in range(2)]
            descs += [(qc, [(0, 128, qT[:, qc, :])], f"qT{qc}", 0, None) for qc in range(8)]
            qk_pipeline(descs, 512, tcol, zlist=range(8))
            ntl = main_tiles(m + 1) if not last else None
            attention_stream(m, tl, ntl)
            if not last:
                for kc_ in range(2):
                    S.op("pool", (lambda e, kc_=kc_: e.tensor_copy(out=kTp[:, kc_, :, 0:128], in_=kTp[:, kc_, :, 512:640])), reads=[f"kT{kc_}"], writes=[f"kT{kc_}"])
                S.op("pool", (lambda e: e.tensor_copy(out=vTp[:, 0, :, :], in_=vTp[:, 4, :, :])), reads=["vT4", "vT0"], writes=["vT0"])

        if stop_at == 'main':
            finish_debug()
            return nc
        for kc_, tbuf, tkey in ((0, t1, "t1"), (1, t2, "t2")):
            bt = bank()
            for q4 in range(4):
                S.op("pe", (lambda e, kc_=kc_, q4=q4, bt=bt: e.transpose(ps[bt][0:32, q4 * 128:(q4 + 1) * 128], kf32[:, kc_, 32 * q4:32 * q4 + 32], idf[:, :])),
                     reads=["kf32", "idf"], writes=[f"ps{bt}"], signal=(q4 == 3))
            S.op("act", (lambda e, bt=bt, tbuf=tbuf: e.activation(out=tbuf[0:32, 0, :], in_=ps[bt][0:32, :], func=AF.Copy)), reads=[f"ps{bt}"], writes=[tkey])
            S.dma("pool", nkp_d[:, kc_ * 128:(kc_ + 1) * 128].rearrange("(q p) f -> p q f", p=32),
                  tbuf[0:32, 0, :].rearrange("p (q f) -> p q f", f=128), reads=[tkey], sem=f"zo_k{kc_}")
        if stop_at == 'pkvB':
            finish_debug()
            return nc
        S.dma("pool", nvp_d[:, :], vf32[:, :], reads=["vf32"], sem="zo_v")

        if stop_at == 'pkv':
            finish_debug()
            return nc
        TS = NS
        norm_stage([(xsm[0:16, :], "xsm", 16, 0)])
        v_chunks([(xsm[0:16, :], "xsm", 16, 0)], [1], f32_last=True)
        descs = [(8 + kc_, [(0, 64, kTp[0:64, kc_, 0, 128:144]), (64, 128, kTp[64:128, kc_, 1, 128:144])], f"kT{kc_}", 1, (kf32[:, kc_, 0:16], "kf32", 0)) for kc_ in range(2)]
        descs += [(qc, [(0, 128, qT[:, qc, 0:16])], f"qT{qc}", 0, None) for qc in range(8)]
        qk_pipeline(descs, 16, TS, zlist=range(8))
        if stop_at == 'sA':
            finish_debug()
            return nc
        bt = bank()
        for kc_ in range(2):
            S.op("pe", (lambda e, kc_=kc_, bt=bt: e.transpose(ps[bt][0:16, kc_ * 128:(kc_ + 1) * 128], kf32[:, kc_, 0:16], idf[:, :])), reads=["kf32", "idf"], writes=[f"ps{bt}"], signal=(kc_ == 1))
        S.op("act", lambda e, bt=bt: e.activation(out=t2[0:16, 0, 0:256], in_=ps[bt][0:16, 0:256], func=AF.Copy), reads=[f"ps{bt}"], writes=["t2"])
        S.dma("pool", nks_d[:, 127, :], t2[0:16, 0, 0:256], reads=["t2"], sem="outs")
        S.dma("pool", nvs_d[:, 127, :], vf32[0:16, :], reads=["vf32"], sem="outs")
        S.dma("sp", nks_d[:, 0:127, :], ck_d[:, 1:128, :], sem="outs2")
        S.dma("sp", nvs_d[:, 0:127, :], cv_d[:, 1:128, :], sem="outs2")
        S.op("act", lambda e: e.activation(out=vsb[0:16, :], in_=vf32[0:16, :], func=AF.Copy), reads=["vf32"], writes=["vsb"])
        if stop_at == 'sB':
            finish_debug()
            return nc
        ogs = og[:, 0, :].rearrange("p (k q) -> p k q", q=128)
        for gi in range(4):
            b0 = 4 * gi
            ckv = wr[:, 0:2, :].rearrange("p a b -> p (a b)").bitcast(F32).rearrange("p (b f) -> p b f", f=256)
            cvv = wr[:, 2:4, :].rearrange("p a b -> p (a b)").bitcast(F32).rearrange("p (b f) -> p b f", f=256)
            S.dma("sp", ckv, ck_d[b0:b0 + 4].rearrange("b k f -> k b f"), writes=["wr0", "wr1"], sem="wr0")
            S.dma("sp", cvv, cv_d[b0:b0 + 4].rearrange("b k f -> k b f"), writes=["wr2", "wr3"], sem="wr2")
            S.op("dve", lambda e: e.tensor_copy(out=ckb, in_=ckv), reads=["wr0", "wr1"], writes=["ckb", "mask0", "mask1", "mask2", "sinkexp"])
            S.op("act", lambda e: e.activation(out=vsg, in_=cvv, func=AF.Copy), reads=["wr2", "wr3"], writes=["vsg", "mask0", "mask1", "mask2", "sinkexp"])
            S.dma("sp", vsg[0:1, :, :], vsb[b0:b0 + 4, :], reads=["vsb", "vsg"], writes=["vsg"], sem="vsg")
            if stop_at == 'g0a':
                finish_debug()
                return nc
            bt = bank()
            pst = ps[bt][:, :].bitcast(BF16).rearrange("p (k t) -> p k t", t=128)
            for bi in range(4):
                for c in range(2):
                    S.op("pe", (lambda e, bi=bi, c=c, pst=pst: e.transpose(pst[:, 2 * bi + c, :], ckb[:, bi, c * 128:(c + 1) * 128], id16[:, :])),
                         reads=["ckb", "id16"], writes=[f"ps{bt}"], signal=(bi == 3 and c == 1))
            S.op("act", (lambda e, pst=pst: e.activation(out=ktg, in_=pst, func=AF.Copy)), reads=[f"ps{bt}"], writes=["ktg", "mask0", "mask1", "mask2", "sinkexp"])
            for half in range(2):
                S.op("dve", (lambda e, b0=b0, half=half: e.tensor_copy(
                    out=ktg.rearrange("p (b c) k -> p c b k", c=2)[half * 64:(half + 1) * 64, :, :, 0],
                    in_=kTp[half * 64:(half + 1) * 64, :, half, 128 + b0:128 + b0 + 4])),
                    reads=["ktg", "kT0", "kT1"], writes=["ktg"])
            if stop_at == 'g0b':
                finish_debug()
                return nc
            bsb2 = [bank(), bank()]
            for half in range(2):
                for bi in range(4):
                    for c in range(2):
                        last_ = (bi == 3 and c == 1)
                        S.op("pe", (lambda e, bi=bi, c=c, half=half, b0=b0, bsb2=bsb2: e.matmul(
                            ps[bsb2[half]][:, (bi * 2 + c) * 4:(bi * 2 + c) * 4 + 4], lhsT=ktg[half * 64:(half + 1) * 64, 2 * bi + c, :],
                            rhs=qT[half * 64:(half + 1) * 64, 4 * c:4 * c + 4, b0 + bi], start=True, stop=True)),
                            reads=["ktg"] + qkeys, writes=[f"ps{bsb2[half]}"], signal=last_)
            for half in range(2):
                S.op("act", (lambda e, half=half, bsb2=bsb2: e.activation(
                    out=pts[:, 0:64].rearrange("p (b c h g) -> p b c h g", c=2, h=2, g=4)[:, :, :, half, :],
                    in_=ps[bsb2[half]][:, 0:32].rearrange("p (b c g) -> p b c g", c=2, g=4), func=AF.Exp, scale=0.125)),
                    reads=[f"ps{bsb2[half]}"], writes=["pts"])
            if stop_at == 'g0c':
                finish_debug()
                return nc
            bdn = bank()
            mm_group(ps[bdn][:, 0:64], [(ones16[:, :], pts[:, 0:64])], ["pts", "ones16"], f"ps{bdn}")
            bos = bank()
            for bi in range(4):
                for c in range(2):
                    for half in range(2):
                        h = 2 * c + half
                        last_ = (bi == 3 and c == 1 and half == 1)
                        S.op("pe", (lambda e, bi=bi, c=c, half=half, h=h, bos=bos: e.matmul(
                            ps[bos][half * 64:(half + 1) * 64, bi * 8 + c * 4:bi * 8 + c * 4 + 4], lhsT=vsg[:, bi, h * 64:(h + 1) * 64],
                            rhs=pts[:, (bi * 4 + h) * 4:(bi * 4 + h) * 4 + 4], start=True, stop=True)),
                            reads=["vsg", "pts"], writes=[f"ps{bos}"], signal=last_)
            if stop_at == 'g0d':
                finish_debug()
                return nc
            for half in range(2):
                S.op("dve", (lambda e, half=half, bdn=bdn: e.tensor_tensor(
                    out=rsm[half * 64:(half + 1) * 64, :].rearrange("p (b c g) -> p b c g", c=2, g=4),
                    in0=ps[bdn][half * 64:(half + 1) * 64, 0:64].rearrange("p (b c h g) -> p b c h g", c=2, h=2, g=4)[:, :, :, half, :],
                    in1=sk16[half * 64:(half + 1) * 64, :].rearrange("p (b c g) -> p b c g", c=2, g=4), op=ALU.add)),
                    reads=[f"ps{bdn}", "sk16"], writes=["rsm"])
            S.op("dve", lambda e: e.reciprocal(out=rsm[:, :], in_=rsm[:, :]), reads=["rsm"], writes=["rsm"])
            S.op("dve", (lambda e, bos=bos: e.tensor_tensor(out=ossm[:, :], in0=ps[bos][:, 0:32], in1=rsm[:, :], op=ALU.mult)), reads=[f"ps{bos}", "rsm"], writes=["ossm"])
            S.op("dve", (lambda e, b0=b0: e.tensor_tensor(
                out=ogs[:, :, b0:b0 + 4], in0=ossm[:, :].rearrange("p (b k) -> p k b", k=8), in1=zs[:, :, b0:b0 + 4], op=ALU.mult)),
                reads=["ossm"] + [f"zs{i}" for i in range(8)], writes=["og0"])
            if stop_at == 'g0e':
                finish_debug()
                return nc
        if stop_at == 'sE':
            finish_debug()
            return nc
        out_proj(0, 16, xsm[0:16, :], "xsm")
        S.dma("pool", ys_d[:, :], xsm[0:16, :], reads=["xsm"], sem="outs")

        S.wait_all("sp", [k for k in S.sems if k == "outs2" or k.startswith("wscr")])
        S.wait_all("pool", ["outs", "wnatA"] + [k for k in S.sems if k.startswith("zo_")])
        S.wait_all("act", ["tabs"])
        S.replay()
    return nc


def _core_inputs(c, NMT, inp):
    NT = 4 * NMT
    L = 128 * NT
    b, j = c // 4, c % 4
    xp = inp["x_prompt"]
    start = j * L
    xin = np.zeros((144 + L, 1024), np.float32)
    if j > 0:
        xin[0:144] = xp[b, start - 144:start]
    xin[144:] = xp[b, start:start + L]
    sl = slice(16 * c, 16 * c + 16)
    d = {
        "xin": xin,
        "xs": np.ascontiguousarray(inp["x_sample"][sl, 0, :]),
        "spool": np.ascontiguousarray(inp["state_pool"][0, sl].reshape(240, 512)),
        "sconv": np.ascontiguousarray(inp["state_conv"][0, sl].reshape(32, 512)),
        "ck": np.ascontiguousarray(inp["cache_k"][0, sl].reshape(16, 128, 256)),
        "cv": np.ascontiguousarray(inp["cache_v"][0, sl].reshape(16, 128, 256)),
        "norm_g": np.ascontiguousarray(inp["norm_g"]),
        "w_in_even": np.ascontiguousarray(inp["w_in_even"][0]),
        "pool_w": np.ascontiguousarray(inp["pool_w"][0]),
        "pool_scale": np.ascontiguousarray(inp["pool_scale"][0:1]),
        "conv_w": np.ascontiguousarray(inp["conv_w"][0]),
        "w_out_even": np.ascontiguousarray(inp["w_out_even"][0]),
        "w_in_odd": np.ascontiguousarray(inp["w_in_odd"][0]),
        "q_norm_g": np.ascontiguousarray(inp["q_norm_g"][0:1]),
        "k_norm_g": np.ascontiguousarray(inp["k_norm_g"][0:1]),
        "attn_sinks": np.ascontiguousarray(inp["attn_sinks"][0:1]),
        "w_out_odd": np.ascontiguousarray(inp["w_out_odd"][0]),
        "meta": np.array([[float(start), 1.0 if j > 0 else 0.0]], np.float32),
    }
    return {k: np.asarray(v, np.float32) for k, v in d.items()}


def kernel(**inputs):
    NMT = 4
    inp = {k: np.asarray(v) for k, v in inputs.items()}
    nc = build(NMT=NMT)
    in_maps = [_core_inputs(c, NMT, inp) for c in range(8)]
    res = run_bass_kernel_spmd(nc, in_maps, core_ids=list(range(8)))
    r = res.results
    L = 512 * NMT
    y = np.zeros((2, 4 * L, 1024), np.float32)
    for c in range(8):
        y[c // 4, (c % 4) * L:(c % 4 + 1) * L] = r[c]["y"]
    ys = np.concatenate([r[c]["ys"] for c in range(8)], 0).reshape(128, 1, 1024)
    npp = np.stack([r[3]["npp"], r[7]["npp"]])[None]
    nps = np.concatenate([r[c]["nps"] for c in range(8)], 0)[None]
    ncp = np.stack([r[3]["ncp"], r[7]["ncp"]])[None]
    ncs = np.concatenate([r[c]["ncs"] for c in range(8)], 0)[None]
    nkp = np.stack([r[3]["nkp"], r[7]["nkp"]]).reshape(1, 2, 128, 4, 64)
    nvp = np.stack([r[3]["nvp"], r[7]["nvp"]]).reshape(1, 2, 128, 4, 64)
    nks = np.concatenate([r[c]["nks"] for c in range(8)], 0).reshape(1, 128, 128, 4, 64)
    nvs = np.concatenate([r[c]["nvs"] for c in range(8)], 0).reshape(1, 128, 128, 4, 64)
    f = lambda a: np.ascontiguousarray(a, dtype=np.float32)
    return (f(y), f(ys), f(npp), f(nps), f(ncp), f(ncs), f(nkp), f(nvp), f(nks), f(nvs))
```

```python
import contextlib
import math
import os
import numpy as np
import concourse.bass as bass
import concourse.mybir as mybir
from concourse.bass_utils import run_bass_kernel_spmd

F32 = mybir.dt.float32
BF16 = mybir.dt.bfloat16
I32 = mybir.dt.int32
ALU = mybir.AluOpType
AF = mybir.ActivationFunctionType

ENG = ["pe", "act", "dve", "pool", "sp"]
PAST_LEN = 16384
RMS_EPS = 1e-6


class Sched:
    def __init__(self, nc, stack):
        self.nc = nc
        self.stack = stack
        self.prog = {e: [] for e in ENG}
        self.sems = {}
        self.cnt = {}
        self.seen = {e: {} for e in ENG}
        self.lastw = {}
        self.readers = {}
        self.group_sems = {"c0", "c1", "c2", "wgs0", "wgs1", "wgs2", "wOs", "wscrA", "wscrB", "wscrC", "wnatA", "tabs"}
        for e in ENG:
            self._sem("E_" + e)

    def _sem(self, name):
        if name not in self.sems:
            self.sems[name] = self.stack.enter_context(self.nc.semaphore(name))
            self.cnt[name] = 0
        return self.sems[name]

    def _deps(self, reads, writes):
        deps = {}

        def add(d):
            if d is not None:
                s, v = d
                if deps.get(s, 0) < v:
                    deps[s] = v

        for k in reads:
            add(self.lastw.get(k))
            if k.startswith("ps"):
                for d in self.readers.get(k, ()):
                    add(d)
        for k in writes:
            add(self.lastw.get(k))
            for d in self.readers.get(k, ()):
                add(d)
        return deps

    def _waits(self, eng, deps):
        own = "E_" + eng
        for s, v in deps.items():
            if eng == "pe" and s == own:
                continue
            if s in self.group_sems:
                v = self.cnt[s]
            if self.seen[eng].get(s, 0) >= v:
                continue
            self.seen[eng][s] = v
            self.prog[eng].append(("wait", s, v))

    def _record(self, d, reads, writes):
        for k in writes:
            self.lastw[k] = d
            self.readers[k] = []
        for k in reads:
            self.readers.setdefault(k, []).append(d)

    def op(self, eng, fn, reads=(), writes=(), signal=True):
        self._waits(eng, self._deps(reads, writes))
        own = "E_" + eng
        if signal:
            self.cnt[own] += 1
            tick = self.cnt[own]
        else:
            tick = self.cnt[own] + 1
        self.prog[eng].append(("op", fn, own if signal else None))
        self._record((own, tick), reads, writes)

    def dma(self, q, out, in_, reads=(), writes=(), sem="dma", **kw):
        self._sem(sem)
        self._waits(q, self._deps(reads, writes))
        self.cnt[sem] += 16
        self.prog[q].append(("dma", out, in_, sem, kw))
        self._record((sem, self.cnt[sem]), reads, writes)

    def wait_all(self, eng, sems):
        for s in sems:
            v = self.cnt[s]
            if v > 0 and self.seen[eng].get(s, 0) < v:
                self.seen[eng][s] = v
                self.prog[eng].append(("wait", s, v))

    def barrier(self):
        names = list(self.sems.keys())
        for e in ENG:
            self.wait_all(e, [n for n in names if not (e == "pe" and n == "E_pe")])

    def replay(self):
        sems = self.sems

        def run(eng_obj, items):
            for it in items:
                if it[0] == "wait":
                    eng_obj.wait_ge(sems[it[1]], it[2])
                elif it[0] == "op":
                    ins = it[1](eng_obj)
                    if it[2] is not None:
                        ins.then_inc(sems[it[2]], 1)
                else:
                    _, out, in_, s, kw = it
                    eng_obj.dma_start(out=out, in_=in_, **kw).then_inc(sems[s], 16)

        with self.nc.Block() as block:
            @block.tensor
            def _(e):
                run(e, self.prog["pe"])

            @block.scalar
            def _(e):
                run(e, self.prog["act"])

            @block.vector
            def _(e):
                run(e, self.prog["dve"])

            @block.gpsimd
            def _(e):
                run(e, self.prog["pool"])

            @block.sync
            def _(e):
                run(e, self.prog["sp"])


def build(NMT=4, stop_after_l0=False, stop_at=None):
    NT = 4 * NMT
    NS = 144 + 128 * NT
    nc = bass.Bass("TRN2", target_bir_lowering=False)

    def din(n, s):
        return nc.dram_tensor(n, list(s), F32, kind="ExternalInput").ap()

    def dout(n, s):
        return nc.dram_tensor(n, list(s), F32, kind="ExternalOutput").ap()

    xin = din("xin", [NS, 1024]); xs_d = din("xs", [16, 1024])
    spool_d = din("spool", [240, 512]); sconv_d = din("sconv", [32, 512])
    ck_d = din("ck", [16, 128, 256]); cv_d = din("cv", [16, 128, 256])
    ng_d = din("norm_g", [2, 1024]); wie_d = din("w_in_even", [1024, 3072])
    pw_d = din("pool_w", [4, 128, 128]); psc_d = din("pool_scale", [1, 512])
    cw_d = din("conv_w", [3, 512]); woe_d = din("w_out_even", [1024, 1024])
    wio_d = din("w_in_odd", [1024, 2560]); qg_d = din("q_norm_g", [1, 64]); kg_d = din("k_norm_g", [1, 64])
    sk_d = din("attn_sinks", [1, 16]); woo_d = din("w_out_odd", [1024, 1024]); meta_d = din("meta", [1, 2])

    y_d = dout("y", [128 * NT, 1024]); ys_d = dout("ys", [16, 1024])
    npp_d = dout("npp", [15, 512]); nps_d = dout("nps", [16, 15, 512])
    ncp_d = dout("ncp", [2, 512]); ncs_d = dout("ncs", [16, 2, 512])
    nkp_d = dout("nkp", [128, 256]); nvp_d = dout("nvp", [128, 256])
    nks_d = dout("nks", [16, 128, 256]); nvs_d = dout("nvs", [16, 128, 256])
    wscr = nc.dram_tensor("wscr", [28, 128, 1024], BF16).ap()
    tabs = nc.dram_tensor("tabs", [2, 128, NS + 16], F32).ap()
    wnat = nc.dram_tensor("wnat", [1024, 2560], BF16).ap()
    wonat = nc.dram_tensor("wonat", [1024, 1024], BF16).ap()

    with contextlib.ExitStack() as stack:
        S = Sched(nc, stack)

        def sb(n, s, d=F32):
            return stack.enter_context(nc.sbuf_tensor(n, list(s), d))

        x1s = sb("x1s", [128, NT + 1, 1024])
        xsm = sb("xsm", [16, 1024]); xpre = sb("xpre", [128, 1024])
        wO = sb("wO", [128, 8, 1024], BF16)
        stg = sb("stg", [128, 2, 1024])
        gt = sb("gt", [128, 1024])
        hT = sb("hT", [128, 8, 512], BF16)
        hn = sb("hn", [128, 2, 1024], BF16)
        id16 = sb("id16", [128, 128], BF16); idf = sb("idf", [128, 128])
        ssq = sb("ssq", [128, 4]); ms = sb("ms", [128, 4]); rstd = sb("rstd", [128, 4]); nhalf = sb("nhalf", [128, 4])
        metab = sb("metab", [128, 2])
        perm16 = sb("perm16", [128, 128], BF16); blk64 = sb("blk64", [128, 128], BF16); ones16 = sb("ones16", [128, 128], BF16)
        gq2 = sb("gq2", [128, 2]); skb = sb("skb", [128, 16]); ske = sb("ske", [128, 16])
        sgn = sb("sgn", [128, 1]); invf = sb("invf", [128, 1]); expo = sb("expo", [128, 1]); base1e4 = sb("base1e4", [128, 1])
        vf32 = sb("vf32", [128, 256])
        epsb = sb("epsb", [128, 1]); oneb = sb("oneb", [128, 1])
        gsm = sb("gsm", [128, 16])
        tmpi = sb("tmpi", [128, 512], I32)
        SCRW = 22200
        scr = sb("scr", [128, SCRW])
        psbig = stack.enter_context(nc.psum_tensor("psbig", [128, 4096], F32))
        ps = [psbig[:, i * 512:(i + 1) * 512] for i in range(8)]

        class Carver:
            def __init__(self):
                self.off = 0

            def alloc(self, shape, dt=F32):
                n = int(np.prod(shape))
                nb = n * (4 if dt in (F32, I32) else 2)
                nb4 = (nb + 3) // 4
                assert self.off + nb4 <= SCRW, ("scratch overflow", self.off + nb4, SCRW)
                ap = scr[:, self.off:self.off + nb4]
                self.off += nb4
                if dt != F32:
                    ap = ap.bitcast(dt)
                if dt == BF16 and nb % 4:
                    ap = ap[:, 0:n]
                if len(shape) == 2:
                    return ap.rearrange("p (a b) -> p a b", b=shape[1])
                if len(shape) == 3:
                    return ap.rearrange("p (a b c) -> p a b c", b=shape[1], c=shape[2])
                return ap

        bank_rr = [0]

        held = set()

        def bank(hold=False):
            for k in range(8):
                i = (bank_rr[0] + k) % 8
                if i not in held:
                    bank_rr[0] = (i + 1) % 8
                    if hold:
                        held.add(i)
                    return i
            raise RuntimeError("no free PSUM bank")

        def release(i):
            held.discard(i)

        def bank_pair():
            for k in range(4):
                i = (2 * ((bank_rr[0] + 1) // 2) + 2 * k) % 8
                if i not in held and (i + 1) not in held:
                    held.add(i); held.add(i + 1)
                    bank_rr[0] = (i + 2) % 8
                    return i
            raise RuntimeError("no free PSUM bank pair")

        def mm_group(out_ap, pairs, reads, bkey):
            n = len(pairs)
            for i, (l, r) in enumerate(pairs):
                S.op("pe", (lambda e, l=l, r=r, i=i: e.matmul(out_ap, lhsT=l, rhs=r, start=(i == 0), stop=(i == n - 1))),
                     reads=reads, writes=[bkey], signal=(i == n - 1))

        S.op("pool", lambda e: e.iota(tmpi[:, 0:128], pattern=[[1, 128]], base=0, channel_multiplier=-1), writes=["tmpi"])
        S.op("dve", lambda e: e.tensor_copy(out=idf[:], in_=tmpi[:, 0:128]), reads=["tmpi"], writes=["idf"])
        S.op("dve", lambda e: e.tensor_single_scalar(out=id16[:], in_=idf[:], scalar=0.0, op=ALU.is_equal), reads=["idf"], writes=["id16"])
        S.op("dve", lambda e: e.tensor_single_scalar(out=idf[:], in_=idf[:], scalar=0.0, op=ALU.is_equal), reads=["idf"], writes=["idf"])
        S.op("pool", lambda e: e.memset(nhalf[:], -0.5), writes=["nhalf"])
        S.op("pool", lambda e: e.memset(ssq[:], 1.0), writes=["ssq0", "ssq1", "ssq2", "ssq3"])
        S.dma("sp", metab[:], meta_d[0:1, :].partition_broadcast(128), writes=["metab"], sem="c0")
        S.dma("sp", gsm[:, 0:4], psc_d[0, :].rearrange("(g p) -> p g", p=128), writes=["gsm"], sem="c0", allow_slow_non_contiguous=True)
        S.dma("sp", gsm[:, 4:16].rearrange("p (k g) -> p k g", g=4), cw_d.rearrange("k (g p) -> p k g", p=128), writes=["gsm"], sem="c0", allow_slow_non_contiguous=True)
        S.dma("sp", gt[:], ng_d[0:1, :].partition_broadcast(128), writes=["gt"], sem="c0")
        for half in range(2):
            S.dma("sp", gq2[half * 64:(half + 1) * 64, 0:1], qg_d[0, :].rearrange("(d o) -> d o", o=1), writes=["gq2"], sem="c0", allow_slow_non_contiguous=True)
            S.dma("sp", gq2[half * 64:(half + 1) * 64, 1:2], kg_d[0, :].rearrange("(d o) -> d o", o=1), writes=["gq2"], sem="c0", allow_slow_non_contiguous=True)
        S.dma("sp", skb[:], sk_d[0:1, :].partition_broadcast(128), writes=["skb"], sem="c0")
        for q4 in range(4):
            tgt = 32.0 if q4 % 2 == 0 else -32.0
            S.op("dve", (lambda e, q4=q4, tgt=tgt: e.tensor_single_scalar(out=perm16[32 * q4:32 * q4 + 32, :], in_=tmpi[32 * q4:32 * q4 + 32, 0:128], scalar=tgt, op=ALU.is_equal)),
                 reads=["tmpi"], writes=["perm16"])
        S.op("pool", lambda e: e.memset(blk64[:], 0.0), writes=["blk64"])
        S.op("pool", lambda e: e.memset(blk64[0:64, 0:64], 1.0), reads=["blk64"], writes=["blk64"])
        S.op("pool", lambda e: e.memset(blk64[64:128, 64:128], 1.0), reads=["blk64"], writes=["blk64"])
        S.op("pool", lambda e: e.memset(ones16[:], 1.0), writes=["ones16"])
        for q4 in range(4):
            S.op("pool", (lambda e, q4=q4: e.memset(sgn[32 * q4:32 * q4 + 32, :], -1.0 if q4 % 2 == 0 else 1.0)), writes=["sgn"])
        S.op("pool", lambda e: e.memset(base1e4[:], 10000.0), writes=["base1e4"])
        S.op("pool", lambda e: e.memset(epsb[:], RMS_EPS), writes=["epsb"])
        S.op("pool", lambda e: e.memset(oneb[:], 1.0), writes=["oneb"])
        S.op("pool", lambda e: e.iota(tmpi[:, 168:169], pattern=[[0, 1]], base=0, channel_multiplier=1), writes=["tmpi5"])
        S.op("dve", lambda e: e.tensor_single_scalar(out=tmpi[:, 168:169], in_=tmpi[:, 168:169], scalar=31, op=ALU.bitwise_and), reads=["tmpi5"], writes=["tmpi5"])
        S.op("dve", lambda e: e.tensor_scalar(out=expo[:], in0=tmpi[:, 168:169], scalar1=-1.0 / 32, scalar2=None, op0=ALU.mult), reads=["tmpi5"], writes=["expo"])
        S.op("pool", lambda e: e.tensor_tensor(out=invf[:], in0=base1e4[:], in1=expo[:], op=ALU.pow), reads=["base1e4", "expo"], writes=["invf"])

        C = Carver()
        wbig = C.alloc([8, 3072], BF16)
        poolw = C.alloc([4, 128], BF16)
        ubuf = C.alloc([2, 528]); vbuf = C.alloc([2, 516])
        ucar = C.alloc([4, 16]); vcar = C.alloc([4, 2])
        ycat = C.alloc([8, 512], BF16)
        tA = C.alloc([528])[:, :]; tB = C.alloc([528])[:, :]
        dT = C.alloc([2, 512], BF16)
        sza = C.alloc([1, 512]); ctmp = C.alloc([1, 512]); szb = C.alloc([1, 512]); acc = C.alloc([1, 512])
        _o = C.off
        cbf = C.alloc([2, 1024], BF16)
        C.off = _o
        spl = C.alloc([2, 512])
        rcw = C.alloc([4, 16]); selp = C.alloc([4, 8]); selc = C.alloc([2, 16])
        scv = C.alloc([512])
        lastT = C.alloc([4, 32])
        l0_end = C.off

        S.op("pool", lambda e: e.memset(ucar, 0.0), writes=["ucar0", "ucar1", "ucar2", "ucar3"])
        S.op("pool", lambda e: e.memset(vcar, 0.0), writes=["vcar0", "vcar1", "vcar2", "vcar3"])

        S.op("pool", lambda e: e.iota(tmpi[:, 128:144], pattern=[[1, 16]], base=1, channel_multiplier=0), writes=["tmpi2"])
        for g in range(4):
            w = float(2 ** (g + 1))
            S.op("dve", (lambda e, g=g: e.tensor_copy(out=rcw[:, g, :], in_=tmpi[:, 128:144])), reads=["tmpi2"], writes=["rcw"])
            S.op("dve", (lambda e, g=g, w=w: e.tensor_scalar(out=rcw[:, g, :], in0=rcw[:, g, :], scalar1=metab[:, 0:1], scalar2=w, op0=ALU.add, op1=ALU.min)),
                 reads=["rcw", "metab"], writes=["rcw"])
            S.op("dve", (lambda e, g=g: e.reciprocal(out=rcw[:, g, :], in_=rcw[:, g, :])), reads=["rcw"], writes=["rcw"])
            S.op("dve", (lambda e, g=g, w=w: e.tensor_scalar(out=rcw[:, g, :], in0=rcw[:, g, :], scalar1=w, scalar2=None, op0=ALU.mult)),
                 reads=["rcw"], writes=["rcw"])

        S.op("pool", lambda e: e.iota(tmpi[:, 144:152], pattern=[[-15, 8]], base=0, channel_multiplier=1), writes=["tmpi3"])
        for g in range(4):
            w = 2 ** (g + 1)
            S.op("dve", (lambda e, g=g: e.tensor_copy(out=selp[:, g, :], in_=tmpi[:, 144:152])), reads=["tmpi3", "selp"], writes=["selp"])
            S.op("dve", (lambda e, g=g, w=w: e.tensor_scalar(out=tA[:, 0:8], in0=selp[:, g, :], scalar1=float(16 - w), scalar2=None, op0=ALU.is_ge)),
                 reads=["selp"], writes=["tA"])
            S.op("dve", (lambda e, g=g: e.tensor_scalar(out=selp[:, g, :], in0=selp[:, g, :], scalar1=14.0, scalar2=None, op0=ALU.is_le)),
                 reads=["selp"], writes=["selp"])
            S.op("dve", (lambda e, g=g: e.tensor_tensor(out=selp[:, g, :], in0=selp[:, g, :], in1=tA[:, 0:8], op=ALU.mult)),
                 reads=["selp", "tA"], writes=["selp"])
        S.op("pool", lambda e: e.iota(tmpi[:, 152:168], pattern=[[-2, 16]], base=0, channel_multiplier=1), writes=["tmpi4"])
        for r in range(2):
            S.op("dve", (lambda e, r=r: e.tensor_copy(out=selc[:, r, :], in_=tmpi[:, 152:168])), reads=["tmpi4"], writes=["selc"])
            S.op("dve", (lambda e, r=r: e.tensor_scalar(out=selc[:, r, :], in0=selc[:, r, :], scalar1=float(r), scalar2=None, op0=ALU.is_equal)),
                 reads=["selc"], writes=["selc"])

        def xtile_ap(t):
            return xin[16 + 128 * t:16 + 128 * (t + 1), :]

        S.dma("sp", xpre[0:16, :], xin[0:16, :], writes=["xpre"], sem="xpre")
        S.dma("sp", x1s[:, 0, :], xtile_ap(0), writes=["x1_0"], sem="x1_0")
        S.dma("sp", xsm[:], xs_d[:, :], writes=["xsm"], sem="xsm")
        S.dma("sp", spl[0:120, 0, :], spool_d[0:120, :], writes=["cbf0"], sem="c1")
        S.dma("sp", spl[0:120, 1, :], spool_d[120:240, :], writes=["cbf1"], sem="c1")
        S.dma("sp", scv[0:32, :], sconv_d[:, :], writes=["scv"], sem="c1")

        stg_rr = [0]

        cast_rr = [0]

        def load_cast(dst, src, dkey, shape3=None):
            s = stg_rr[0]
            stg_rr[0] ^= 1
            view = stg[:, s, :]
            if shape3 is not None:
                view = view.rearrange("p (a b) -> p a b", b=shape3)
            S.dma("sp", view, src, writes=[f"stg{s}"], sem=f"stg{s}")
            cast_rr[0] ^= 1
            if cast_rr[0]:
                S.op("dve", lambda e: e.tensor_copy(out=dst, in_=view), reads=[f"stg{s}"], writes=[dkey])
            else:
                S.op("act", lambda e: e.activation(out=dst, in_=view, func=AF.Copy), reads=[f"stg{s}"], writes=[dkey])

        def load_pw():
            s = stg_rr[0]
            stg_rr[0] ^= 1
            view = stg[:, s, 0:512].rearrange("p (g e) -> p g e", e=128)
            S.dma("sp", view, pw_d.rearrange("g c e -> c g e"), writes=[f"stg{s}"], sem=f"stg{s}")
            S.op("pool", lambda e: e.tensor_copy(out=poolw, in_=view), reads=[f"stg{s}"], writes=["poolw"])

        fc_order = []
        for g in range(4):
            fc_order += [g, 4 + g]
        for i in range(4):
            fc_order += [12 + i, 16 + i, 8 + i, 20 + i]
        pend = []
        pend.append(lambda: S.dma("pool", poolw, pw_d.rearrange("g c e -> c g e"), writes=["poolw"], sem="pws"))
        xt_list = list(range(1, NT + 1))

        def load_wgroup(cg, kc):
            S.dma("pool", wbig[:, kc, cg * 1024:(cg + 1) * 1024], wie_d[kc * 128:(kc + 1) * 128, cg * 1024:(cg + 1) * 1024],
                  writes=[f"wg{cg}_{kc}"], sem=f"wgs{cg}")

        def xload(t, gate):
            S.dma("sp", x1s[:, t, :], xtile_ap(t), reads=[gate], writes=[f"x1_{t}"], sem=f"x1_{t}")

        for cg in range(3):
            for kc in range(8):
                pend.append((lambda cg=cg, kc=kc: load_wgroup(cg, kc)))
            if cg == 0:
                for _ in range(4):
                    if xt_list:
                        pend.append((lambda t=xt_list.pop(0): xload(t, "wg0_7")))
        for kc in range(8):
            pend.append((lambda kc=kc: S.dma("pool", wO[:, kc, :], woe_d[kc * 128:(kc + 1) * 128, :], writes=[f"wO{kc}"], sem="wOs")))
        for t in xt_list:
            pend.append((lambda t=t: xload(t, "wg2_7")))

        def l1w_cast(kc):
            def go():
                S.dma("pool", wnat[kc * 128:(kc + 1) * 128, :], wio_d[kc * 128:(kc + 1) * 128, :], writes=[f"wnat{kc}"], sem="wnatA", max_dma_last_dim=4096)
                S.dma("pool", wonat[kc * 128:(kc + 1) * 128, :], woo_d[kc * 128:(kc + 1) * 128, :], writes=[f"wonat{kc}"], sem="wnatA", max_dma_last_dim=4096)
            return go

        NATK = [f"wnat{kc}" for kc in range(8)]
        ONATK = [f"wonat{kc}" for kc in range(8)]

        def l1w_thunk(idx):
            def go():
                dst = wscr[idx]
                if idx < 8 or 12 <= idx < 20:
                    qc = idx if idx < 8 else idx - 12
                    base = 0 if idx < 8 else 1536
                    c, g = qc // 4, qc % 4
                    for half in range(2):
                        o = base + (8 * c + 4 * half + g) * 64
                        S.dma("sp", dst.rearrange("p (k h d) -> p k h d", h=2, d=64)[:, :, half, :],
                              wnat[:, o:o + 64].rearrange("(k p) d -> p k d", p=128), reads=NATK, writes=[f"wscr{idx}"], sem="wscrB")
                elif idx < 12:
                    o = 1024 + (idx - 8) * 128
                    S.dma("sp", dst.rearrange("p (k c) -> p k c", c=128), wnat[:, o:o + 128].rearrange("(k p) c -> p k c", p=128),
                          reads=NATK, writes=[f"wscr{idx}"], sem="wscrA")
                else:
                    qc = idx - 20
                    c, g = qc // 4, qc % 4
                    for half in range(2):
                        r0 = (8 * c + 4 * half + g) * 64
                        S.dma("sp", dst[half * 64:(half + 1) * 64, :], wonat[r0:r0 + 64, :], reads=ONATK, writes=[f"wscr{idx}"], sem="wscrC")
            return go

        l1_order = [8, 9, 10, 11] + list(range(20, 28)) + list(range(0, 8)) + list(range(12, 20))

        def prefetch(n):
            for _ in range(n):
                if pend:
                    pend.pop(0)()

        prefetch(38)

        hn_rr = [0]

        def norm_stage(tiles):
            nt_ = len(tiles)
            slots = []
            for j, (src, key, ntok, col0) in enumerate(tiles):
                s = hn_rr[0]; hn_rr[0] ^= 1
                slots.append(s)
                S.op("act", (lambda e, src=src, ntok=ntok, j=j, s=s: e.activation(out=hn[0:ntok, s, :], in_=src, func=AF.Square, accum_out=ssq[0:ntok, j:j + 1])),
                     reads=[key], writes=[f"ssq{j}", f"hn{s}"])
            S.op("dve", lambda e: e.tensor_scalar(out=ms[:, 0:nt_], in0=ssq[:, 0:nt_], scalar1=1.0 / 1024, scalar2=RMS_EPS, op0=ALU.mult, op1=ALU.add),
                 reads=[f"ssq{j}" for j in range(nt_)], writes=[f"ms{j}" for j in range(nt_)])
            S.op("pool", lambda e: e.tensor_tensor(out=rstd[:, 0:nt_], in0=ms[:, 0:nt_], in1=nhalf[:, 0:nt_], op=ALU.pow),
                 reads=[f"ms{j}" for j in range(nt_)] + ["nhalf"], writes=[f"rstd{j}" for j in range(nt_)])
            for j, (src, key, ntok, col0) in enumerate(tiles):
                s = slots[j]
                S.op("dve", (lambda e, src=src, ntok=ntok, j=j, s=s: e.scalar_tensor_tensor(out=hn[0:ntok, s, :], in0=src, scalar=rstd[0:ntok, j:j + 1], in1=gt[0:ntok, :], op0=ALU.mult, op1=ALU.mult)),
                     reads=[key, f"rstd{j}", "gt"], writes=[f"hn{s}"])
                b = bank()
                pst = ps[b][:, :].bitcast(BF16).rearrange("p (k t) -> p k t", t=128)
                for kc in range(8):
                    S.op("pe", (lambda e, kc=kc, ntok=ntok, s=s, pst=pst: e.transpose(pst[:, kc, 0:ntok], hn[0:ntok, s, kc * 128:(kc + 1) * 128], id16[0:ntok, 0:ntok])),
                         reads=[f"hn{s}", "id16"], writes=[f"ps{b}"], signal=(kc == 7))
                S.op("act", (lambda e, pst=pst, ntok=ntok, col0=col0: e.activation(out=hT[:, :, col0:col0 + ntok], in_=pst[:, :, 0:ntok], func=AF.Copy)),
                     reads=[f"ps{b}"], writes=[f"hT{j}"])
            return ["hT0", "hT1", "hT2", "hT3"]

        def norm_stats(tiles4):
            junk = xpre[:, 512:1024].bitcast(BF16)
            for j, (src, key, ntok, col0) in enumerate(tiles4):
                S.op("act", (lambda e, src=src, ntok=ntok, j=j: e.activation(out=junk[0:ntok, :], in_=src, func=AF.Square, accum_out=ssq[0:ntok, j:j + 1])),
                     reads=[key], writes=[f"ssq{j}", "zt1"])
            S.op("dve", lambda e: e.tensor_scalar(out=ms[:, 0:4], in0=ssq[:, 0:4], scalar1=1.0 / 1024, scalar2=RMS_EPS, op0=ALU.mult, op1=ALU.add),
                 reads=[f"ssq{j}" for j in range(4)], writes=[f"ms{j}" for j in range(4)])
            S.op("pool", lambda e: e.tensor_tensor(out=rstd[:, 0:4], in0=ms[:, 0:4], in1=nhalf[:, 0:4], op=ALU.pow),
                 reads=[f"ms{j}" for j in range(4)] + ["nhalf"], writes=[f"rstd{j}" for j in range(4)])

        def norm_pre(tile, j):
            src, key, ntok, col0 = tile
            s = hn_rr[0]; hn_rr[0] ^= 1
            S.op("dve", (lambda e: e.scalar_tensor_tensor(out=hn[0:ntok, s, :], in0=src, scalar=rstd[0:ntok, j:j + 1], in1=gt[0:ntok, :], op0=ALU.mult, op1=ALU.mult)),
                 reads=[key, f"rstd{j}", "gt"], writes=[f"hn{s}"])
            return s

        def norm_post(tile, j, s):
            src, key, ntok, col0 = tile
            b = bank()
            pst = ps[b][:, :].bitcast(BF16).rearrange("p (k t) -> p k t", t=128)
            for kc in range(8):
                S.op("pe", (lambda e, kc=kc: e.transpose(pst[:, kc, 0:ntok], hn[0:ntok, s, kc * 128:(kc + 1) * 128], id16[0:ntok, 0:ntok])),
                     reads=[f"hn{s}", "id16"], writes=[f"ps{b}"], signal=(kc == 7))
            S.op("act", (lambda e: e.activation(out=hT[:, :, col0:col0 + ntok], in_=pst[:, :, 0:ntok], func=AF.Copy)),
                 reads=[f"ps{b}"], writes=[f"hT{j}"])

        pf_rate = [2]
        slot2 = {"u": 0, "v": 0, "d": 0, "za": 0, "ct": 0, "zb": 0}

        def nxt(k):
            if k in ("za", "ct", "zb"):
                return 0
            s = slot2[k]
            slot2[k] ^= 1
            return s

        def l0_mt(tiles, N, out_specs, sample=False, first_main=False, prenormed=False, next_tiles=None, delay_bd=False, hooks={}):
            pend_bd = [None]
            if prenormed:
                hkeys = ["hT0", "hT1", "hT2", "hT3"]
            else:
                hkeys = norm_stage(tiles)

            def inproj(fc):
                prefetch(pf_rate[0])
                b = bank()
                mm_group(ps[b][:, 0:N], [(wbig[:, kc, fc * 128:(fc + 1) * 128], hT[:, kc, 0:N]) for kc in range(8)], hkeys + [f"wg{fc // 8}_{kc}" for kc in range(8)], f"ps{b}")
                return b

            for g in range(4):
                w = 2 ** (g + 1)
                bu = inproj(g)
                su = nxt("u")
                U = ubuf[:, su, :]
                if not sample:
                    S.op("pool", (lambda e, U=U, g=g: e.tensor_copy(out=U[:, 0:16], in_=ucar[:, g, :])), reads=[f"ucar{g}"], writes=[f"Uc{su}"])
                S.op("act", (lambda e, U=U, bu=bu: e.activation(out=U[:, 16:16 + N], in_=ps[bu][:, 0:N], func=AF.Copy)), reads=[f"ps{bu}"], writes=[f"Un{su}"])
                bz = inproj(4 + g)
                if delay_bd:
                    sz = g % 2
                    sza_ap = sza[:, 0, 0:N] if sz == 0 else xpre[:, 0:N]
                    szk = [f"sza{sz}"] + (["xpre"] if sz == 1 else [])
                else:
                    sz = 0
                    sza_ap = sza[:, 0, 0:N]
                    szk = ["sza0"]
                S.op("act", (lambda e, sza_ap=sza_ap, bz=bz: e.activation(out=sza_ap, in_=ps[bz][:, 0:N], func=AF.Silu)), reads=[f"ps{bz}"], writes=szk)
                if pend_bd[0] is not None:
                    pend_bd[0]()
                    pend_bd[0] = None
                sd = nxt("d")
                if not sample:
                    S.op("pool", (lambda e, U=U, g=g: e.tensor_copy(out=ucar[:, g, :], in_=U[:, N:N + 16])), reads=[f"Un{su}", f"Uc{su}"], writes=[f"ucar{g}"])
                    cur, curkey = U, None
                    src, dst = None, None
                    bufs = [(tA, "tA"), (tB, "tB")]
                    prev, prevkeys = U, [f"Un{su}", f"Uc{su}"]
                    sh = 1
                    for lvl in range(g + 1):
                        ob, okey = bufs[lvl % 2]
                        lo = 2 * sh - 1
                        S.op("dve" if g >= 2 else "pool", (lambda e, ob=ob, prev=prev, sh=sh, lo=lo: e.tensor_tensor(out=ob[:, lo:16 + N], in0=prev[:, lo:16 + N], in1=prev[:, lo - sh:16 + N - sh], op=ALU.add)),
                             reads=prevkeys, writes=[okey])
                        prev, prevkeys = ob, [okey]
                        sh *= 2
                    if first_main:
                        S.op("pool", (lambda e, prev=prev, g=g: e.tensor_tensor(out=prev[:, 16:32], in0=prev[:, 16:32], in1=rcw[:, g, :], op=ALU.mult)),
                             reads=prevkeys + ["rcw"], writes=prevkeys)
                    S.op("dve", (lambda e, prev=prev, U=U, sd=sd, w=w: e.scalar_tensor_tensor(out=dT[:, sd, 0:N], in0=prev[:, 16:16 + N], scalar=1.0 / w, in1=U[:, 16:16 + N], op0=ALU.mult, op1=ALU.subtract)),
                         reads=prevkeys + [f"Un{su}"], writes=[f"dT{sd}"])
                else:
                    bs = bank()
                    for h in range(2):
                        S.op("pe", (lambda e, h=h, g=g, bs=bs: e.matmul(ps[bs][:, 8 * h:8 * h + 8], lhsT=spl[0:120, h, g * 128:(g + 1) * 128], rhs=selp[0:120, g, :], start=True, stop=True)),
                             reads=["cbf0", "cbf1", "selp"], writes=[f"ps{bs}"], signal=(h == 1))
                    S.op("dve", (lambda e, U=U, bs=bs: e.tensor_tensor(out=tA[:, 0:16], in0=ps[bs][:, 0:16], in1=U[:, 16:32], op=ALU.add)),
                         reads=[f"ps{bs}", f"Un{su}"], writes=["tA"])
                    S.op("dve", (lambda e, U=U, sd=sd, w=w: e.scalar_tensor_tensor(out=dT[:, sd, 0:N], in0=tA[:, 0:16], scalar=1.0 / w, in1=U[:, 16:32], op0=ALU.mult, op1=ALU.subtract)),
                         reads=["tA", f"Un{su}"], writes=[f"dT{sd}"])
                    S.op("pool", (lambda e, U=U, g=g: e.tensor_copy(out=lastT[:, g, 0:16], in_=U[:, 16:32])), reads=[f"Un{su}"], writes=["lastT"])
                def blockdiag(g=g, sd=sd, sz=sz, sza_ap=sza_ap):
                    bb = bank()
                    mm_group(ps[bb][:, 0:N], [(poolw[:, g, :], dT[:, sd, 0:N])], [f"dT{sd}", "poolw"], f"ps{bb}")
                    S.op("dve", (lambda e: e.scalar_tensor_tensor(out=ycat[:, g, 0:N], in0=ps[bb][:, 0:N], scalar=gsm[:, g:g + 1], in1=sza_ap, op0=ALU.mult, op1=ALU.mult)),
                         reads=[f"ps{bb}", f"sza{sz}", "gsm"], writes=[f"ycat{g}"])

                if delay_bd:
                    pend_bd[0] = blockdiag
                else:
                    blockdiag()

            for fn in hooks.get("mid", []):
                fn()
            nslots = {}
            if next_tiles is not None:
                norm_stats(next_tiles)
            for i in range(4):
                if i == 2 and next_tiles is not None:
                    nslots[0] = norm_pre(next_tiles[0], 0)
                    nslots[1] = norm_pre(next_tiles[1], 1)
                bc = inproj(12 + i)
                sc = nxt("ct")
                S.op("act", (lambda e, sc=sc, bc=bc: e.activation(out=ctmp[:, sc, 0:N], in_=ps[bc][:, 0:N], func=AF.Copy)), reads=[f"ps{bc}"], writes=[f"ct{sc}"])
                bv = inproj(16 + i)
                if pend_bd[0] is not None:
                    pend_bd[0]()
                    pend_bd[0] = None
                sv = nxt("v")
                V = vbuf[:, sv, :]
                if not sample:
                    S.op("pool", (lambda e, V=V, i=i: e.tensor_copy(out=V[:, 0:2], in_=vcar[:, i, :])), reads=[f"vcar{i}"], writes=[f"Vc{sv}"])
                S.op("dve", (lambda e, V=V, bv=bv, sc=sc: e.tensor_tensor(out=V[:, 2:2 + N], in0=ps[bv][:, 0:N], in1=ctmp[:, sc, 0:N], op=ALU.mult)),
                     reads=[f"ps{bv}", f"ct{sc}"], writes=[f"Vn{sv}"])
                w0 = gsm[:, 4 + i:5 + i]; w1 = gsm[:, 8 + i:9 + i]; w2 = gsm[:, 12 + i:13 + i]
                if not sample:
                    S.op("pool", (lambda e, V=V, i=i: e.tensor_copy(out=vcar[:, i, :], in_=V[:, N:N + 2])), reads=[f"Vn{sv}", f"Vc{sv}"], writes=[f"vcar{i}"])
                    vk = [f"Vn{sv}", f"Vc{sv}"]
                    S.op("dve", (lambda e, V=V, w0=w0: e.tensor_scalar(out=acc[:, 0, 0:N], in0=V[:, 0:N], scalar1=w0, scalar2=None, op0=ALU.mult)), reads=vk + ["gsm"], writes=["acc"])
                    S.op("dve", (lambda e, V=V, w1=w1: e.scalar_tensor_tensor(out=acc[:, 0, 0:N], in0=V[:, 1:1 + N], scalar=w1, in1=acc[:, 0, 0:N], op0=ALU.mult, op1=ALU.add)), reads=vk + ["acc"], writes=["acc"])
                else:
                    bs = bank()
                    for r in range(2):
                        S.op("pe", (lambda e, r=r, i=i, bs=bs: e.matmul(ps[bs][:, 16 * r:16 * r + 16], lhsT=scv[0:32, i * 128:(i + 1) * 128], rhs=selc[0:32, r, :], start=True, stop=True)),
                             reads=["scv", "selc"], writes=[f"ps{bs}"], signal=(r == 1))
                    S.op("dve", (lambda e, bs=bs, w0=w0: e.tensor_scalar(out=acc[:, 0, 0:N], in0=ps[bs][:, 0:16], scalar1=w0, scalar2=None, op0=ALU.mult)), reads=[f"ps{bs}", "gsm"], writes=["acc"])
                    S.op("dve", (lambda e, bs=bs, w1=w1: e.scalar_tensor_tensor(out=acc[:, 0, 0:N], in0=ps[bs][:, 16:32], scalar=w1, in1=acc[:, 0, 0:N], op0=ALU.mult, op1=ALU.add)), reads=[f"ps{bs}", "acc"], writes=["acc"])
                    S.op("pool", (lambda e, V=V, i=i: e.tensor_copy(out=lastT[:, i, 16:32], in_=V[:, 2:18])), reads=[f"Vn{sv}"], writes=["lastT"])
                S.op("dve", (lambda e, V=V, w2=w2: e.scalar_tensor_tensor(out=acc[:, 0, 0:N], in0=V[:, 2:2 + N], scalar=w2, in1=acc[:, 0, 0:N], op0=ALU.mult, op1=ALU.add)), reads=[f"Vn{sv}", "acc"], writes=["acc"])
                bbg = inproj(8 + i)
                S.op("dve", (lambda e, bbg=bbg: e.tensor_tensor(out=acc[:, 0, 0:N], in0=ps[bbg][:, 0:N], in1=acc[:, 0, 0:N], op=ALU.mult)), reads=[f"ps{bbg}", "acc"], writes=["acc"])
                bzb = inproj(20 + i)
                szs = nxt("zb")
                S.op("act", (lambda e, szs=szs, bzb=bzb: e.activation(out=szb[:, szs, 0:N], in_=ps[bzb][:, 0:N], func=AF.Silu)), reads=[f"ps{bzb}"], writes=[f"szb{szs}"])
                S.op("dve", (lambda e, szs=szs, i=i: e.tensor_tensor(out=ycat[:, 4 + i, 0:N], in0=acc[:, 0, 0:N], in1=szb[:, szs, 0:N], op=ALU.mult)), reads=["acc", f"szb{szs}"], writes=[f"ycat{4 + i}"])

            for fn in hooks.get("late", []):
                fn()
            ykeys = [f"ycat{c}" for c in range(8)]
            for oi, (xap, xkey, ntok, col0) in enumerate(out_specs):
                if next_tiles is not None and oi == 0:
                    norm_post(next_tiles[0], 0, nslots[0])
                    norm_post(next_tiles[1], 1, nslots[1])
                    nslots[2] = norm_pre(next_tiles[2], 2)
                    nslots[3] = norm_pre(next_tiles[3], 3)
                if next_tiles is not None and oi == 2:
                    norm_post(next_tiles[2], 2, nslots[2])
                    norm_post(next_tiles[3], 3, nslots[3])
                bs2 = []
                for n2 in range(2):
                    b = bank(hold=True)
                    bs2.append(b)
                    mm_group(ps[b][0:ntok, :], [(ycat[:, kc, col0:col0 + ntok], wO[:, kc, n2 * 512:(n2 + 1) * 512]) for kc in range(8)],
                             ykeys + [f"wO{kc}" for kc in range(8)], f"ps{b}")
                for n2, b in enumerate(bs2):
                    S.op("dve", (lambda e, xap=xap, b=b, ntok=ntok, n2=n2: e.tensor_tensor(out=xap[:, n2 * 512:(n2 + 1) * 512], in0=ps[b][0:ntok, :], in1=xap[:, n2 * 512:(n2 + 1) * 512], op=ALU.add)),
                         reads=[f"ps{b}", xkey], writes=[xkey])
                    release(b)

        l0_mt([(xpre[0:16, :], "xpre", 16, 0), (x1s[:, 0, :], "x1_0", 128, 16)], 144,
              [(x1s[:, 0, :], "x1_0", 128, 16)])
        l0_mt([(xsm[0:16, :], "xsm", 16, 0)], 16, [(xsm[0:16, :], "xsm", 16, 0)], sample=True)
        bt = bank()
        for c in range(4):
            S.op("pe", (lambda e, c=c, bt=bt: e.transpose(ps[bt][0:32, c * 128:(c + 1) * 128], lastT[:, c, :], idf[:, :])), reads=["lastT", "idf"], writes=[f"ps{bt}"], signal=(c == 3))
        S.op("act", lambda e, bt=bt: e.activation(out=tA[0:32, 0:512], in_=ps[bt][0:32, :], func=AF.Copy), reads=[f"ps{bt}"], writes=["tA"])
        S.dma("pool", nps_d[:, 14, :], tA[0:16, 0:512], reads=["tA"], sem="outs")
        S.dma("pool", ncs_d[:, 1, :], tA[16:32, 0:512], reads=["tA"], sem="outs")
        S.dma("sp", nps_d[:, 0:14, :], spool_d.rearrange("(s r) f -> s r f", r=15)[:, 1:15, :], sem="outs2")
        S.dma("sp", ncs_d[:, 0, :], sconv_d.rearrange("(s r) f -> s r f", r=2)[:, 1, :], sem="outs2")
        TWO_PI = 2.0 * math.pi
        C1 = 6.28125
        C2 = TWO_PI - C1
        LIM = 3.1415925
        W = NS + 16

        def table_chunk_stages(c0, n, sample_cols, sl):
            A = stg[:, sl, 0:n]; B = stg[:, sl, 512:512 + n]
            ka, kb_ = f"tbA{sl}", f"tbB{sl}"
            ti = tmpi[:, 0:n]

            def st1():
                if sample_cols:
                    S.op("dve", lambda e: e.memset(A, float(PAST_LEN)), writes=[ka])
                    S.op("dve", lambda e: e.tensor_scalar(out=A, in0=A, scalar1=invf[:, 0:1], scalar2=None, op0=ALU.mult), reads=[ka, "invf"], writes=[ka])
                else:
                    S.op("pool", lambda e: e.iota(ti, pattern=[[1, n]], base=c0 - 144, channel_multiplier=0), writes=["tmpi", "tmpi2", "tmpi3", "tmpi4", "tmpi5"])
                    S.op("dve", lambda e: e.tensor_scalar(out=A, in0=ti, scalar1=metab[:, 0:1], scalar2=invf[:, 0:1], op0=ALU.add, op1=ALU.mult),
                         reads=["tmpi", "metab", "invf"], writes=[ka])
                S.op("dve", lambda e: e.tensor_scalar(out=B, in0=A, scalar1=1.0 / TWO_PI, scalar2=None, op0=ALU.mult), reads=[ka], writes=[kb_])
                S.op("dve", lambda e: e.tensor_copy(out=ti, in_=B), reads=[kb_], writes=["tmpi"])
                S.op("dve", lambda e: e.tensor_copy(out=B, in_=ti), reads=["tmpi"], writes=[kb_])
                S.op("dve", lambda e: e.scalar_tensor_tensor(out=A, in0=B, scalar=-C1, in1=A, op0=ALU.mult, op1=ALU.add), reads=[ka, kb_], writes=[ka])
                S.op("dve", lambda e: e.scalar_tensor_tensor(out=A, in0=B, scalar=-C2, in1=A, op0=ALU.mult, op1=ALU.add), reads=[ka, kb_], writes=[ka])
                S.op("dve", lambda e: e.tensor_scalar(out=A, in0=A, scalar1=LIM, scalar2=-LIM, op0=ALU.min, op1=ALU.max), reads=[ka], writes=[ka])

            def st2():
                S.op("act", lambda e: e.activation(out=B, in_=A, func=AF.Sin), reads=[ka], writes=[kb_])
                S.op("dve", lambda e: e.scalar_tensor_tensor(out=A, in0=A, scalar=-1.0, in1=A, op0=ALU.mult, op1=ALU.max), reads=[ka, kb_], writes=[ka])
                S.op("dve", lambda e: e.tensor_scalar(out=A, in0=A, scalar1=-1.0, scalar2=math.pi / 2, op0=ALU.mult, op1=ALU.add), reads=[ka], writes=[ka])

            def st3():
                S.op("dve", lambda e: e.tensor_scalar(out=B, in0=B, scalar1=sgn[:, 0:1], scalar2=None, op0=ALU.mult), reads=[kb_, "sgn"], writes=[kb_])
                S.dma("act", tabs[1][:, c0:c0 + n], B, reads=[kb_], writes=["tabs"], sem="tabs")
                S.op("act", lambda e: e.activation(out=A, in_=A, func=AF.Sin), reads=[ka], writes=[ka])
                S.dma("act", tabs[0][:, c0:c0 + n], A, reads=[ka], writes=["tabs"], sem="tabs")

            return st1, st2, st3

        tab_chunks = [(c0, min(512, NS - c0), False) for c0 in range(0, NS, 512)] + [(NS, 16, True)]

        for m in range(NMT):
            if m == min(1, NMT - 1):
                for kc in range(8):
                    pend.append(l1w_cast(kc))
                for idx in l1_order:
                    pend.append(l1w_thunk(idx))
            tl = [(x1s[:, 1 + 4 * m + j, :], f"x1_{1 + 4 * m + j}", 128, 128 * j) for j in range(4)]
            ntl0 = [(x1s[:, 5 + 4 * m + j, :], f"x1_{5 + 4 * m + j}", 128, 128 * j) for j in range(4)] if m + 1 < NMT else None
            per = min(2, (len(tab_chunks) + NMT - 1 - m) // (NMT - m))
            hooks = {"mid": [], "late": []}
            for sl in range(per):
                if tab_chunks:
                    st1, st2, st3 = table_chunk_stages(*tab_chunks.pop(0), sl)
                    st1()
                    hooks["mid"].append(st2)
                    hooks["late"].append(st3)
            l0_mt(tl, 512, tl, first_main=(m == 0), prenormed=(m > 0), next_tiles=ntl0, delay_bd=True, hooks=hooks)
        for c in range(4):
            S.op("pool", (lambda e, c=c: e.tensor_copy(out=lastT[:, c, 0:16], in_=ucar[:, c, :])), reads=[f"ucar{c}", "lastT"], writes=["lastT"])
            S.op("pool", (lambda e, c=c: e.tensor_copy(out=lastT[:, c, 16:18], in_=vcar[:, c, :])), reads=[f"vcar{c}", "lastT"], writes=["lastT"])
        bt = bank()
        for c in range(4):
            S.op("pe", (lambda e, c=c, bt=bt: e.transpose(ps[bt][0:32, c * 128:(c + 1) * 128], lastT[:, c, :], idf[:, :])), reads=["lastT", "idf"], writes=[f"ps{bt}"], signal=(c == 3))
        S.op("act", lambda e, bt=bt: e.activation(out=tA[0:32, 0:512], in_=ps[bt][0:32, :], func=AF.Copy), reads=[f"ps{bt}"], writes=["tA"])
        S.dma("pool", npp_d[:, :], tA[1:16, 0:512], reads=["tA"], sem="outs")
        S.dma("pool", ncp_d[:, :], tA[16:18, 0:512], reads=["tA"], sem="outs")

        def finish_debug():
            S.wait_all("sp", [k for k in S.sems if k.startswith("outs") or k.startswith("wscr")])
            S.wait_all("pool", ["outs"])
            S.barrier()
            S.replay()

        if stop_after_l0:
            for t in range(1, NT + 1):
                S.dma("sp", y_d[128 * (t - 1):128 * t, :], x1s[:, t, :], reads=[f"x1_{t}"], sem="outs")
            S.dma("sp", ys_d[:, :], xsm[0:16, :], reads=["xsm"], sem="outs")
            S.wait_all("sp", ["outs"])
            S.replay()
            return nc

        while tab_chunks:
            for st in table_chunk_stages(*tab_chunks.pop(0), 0):
                st()
        prefetch(1000)
        S.barrier()
        W = NS + 16
        C = Carver()
        cosT = C.alloc([W])[:, :]; sinT = C.alloc([W])[:, :]
        wr = C.alloc([6, 1024], BF16)
        qT = C.alloc([8, 512], BF16); kf32 = C.alloc([2, 128])
        kTp = stg[:, :, :].rearrange("p a b -> p (a b)")[:, 0:1280].bitcast(BF16).rearrange("p (c h x) -> p c h x", c=2, h=2)
        onesp = C.alloc([2, 128], BF16)
        zs = C.alloc([8, 512], BF16)
        vTp = C.alloc([5 * 4, 128], BF16).rearrange("p (s h) x -> p s h x", h=4)
        PT = C.alloc([2, 512], BF16)
        _o = C.off
        masks = C.alloc([4, 512], BF16)
        sinkexp = C.alloc([2, 512])
        _e = C.off
        C.off = _o
        ckb = C.alloc([4, 256], BF16); vsg = C.alloc([4, 256], BF16); ktg = C.alloc([8, 128], BF16)
        assert C.off <= _e
        C.off = _e
        q32 = C.alloc([2, 512]); qsq = C.alloc([2, 512], BF16); rs = C.alloc([1, 512]); qn = C.alloc([2, 512], BF16)
        t1 = C.alloc([1, 512]); t2 = C.alloc([1, 512])
        og = C.alloc([2, 1024], BF16)
        rr = C.alloc([1, 512]); ogt = C.alloc([1, 512], BF16)
        zt = C.alloc([1, 512])
        pts = C.alloc([64], BF16)[:, :]; ossm = C.alloc([32])[:, :]; rsm = C.alloc([32])[:, :]; sk16 = C.alloc([32])[:, :]
        vsb = C.alloc([256], BF16)[:, :]

        S.dma("sp", gt[:], ng_d[1:2, :].partition_broadcast(128), writes=["gt"], sem="c2")
        S.op("pool", lambda e: e.memset(kTp, 0.0), writes=["kT0", "kT1", "stg0", "stg1", "tbA0", "tbB0", "tbA1", "tbB1"])
        S.op("pool", lambda e: e.memset(vTp, 0.0), writes=[f"vT{i}" for i in range(5)])
        S.op("pool", lambda e: e.memset(onesp, 0.0), writes=["onesp"])
        S.op("pool", lambda e: e.memset(onesp[:, 0, 0:64], 1.0), reads=["onesp"], writes=["onesp"])
        S.op("pool", lambda e: e.memset(onesp[:, 1, 64:128], 1.0), reads=["onesp"], writes=["onesp"])
        for ch in range(8):
            S.dma("sp", wO[:, ch, :], wscr[20 + ch], reads=[f"wscr{20 + ch}"], writes=[f"wO{ch}"], sem="c2")

        ring_plan = [10, 11, 8, 9]
        _full = [10, 11, 8, 9]
        for i_ in range(8):
            _full += [i_, 12 + i_]
        for m in range(NMT + 1):
            ring_plan += _full
        ring_pos = [0]
        ring_use = [0]

        def ring_issue():
            i = ring_pos[0]
            if i >= len(ring_plan):
                return
            ring_pos[0] += 1
            sl = i % 6
            S.dma("sp", wr[:, sl, :], wscr[ring_plan[i]], reads=[f"wscr{ring_plan[i]}"], writes=[f"wr{sl}"], sem=f"wr{sl}")

        def ring_take(expect):
            i = ring_use[0]
            assert ring_plan[i] == expect, (i, ring_plan[i], expect)
            ring_use[0] += 1
            while ring_pos[0] < min(i + 5, len(ring_plan)):
                ring_issue()
            sl = i % 6
            return wr[:, sl, :].rearrange("p (k c) -> p k c", c=128), f"wr{sl}"

        for _ in range(4):
            ring_issue()

        S.op("pool", lambda e: e.iota(tmpi[:, :], pattern=[[0, 4], [1, 128]], base=0, channel_multiplier=-1), writes=["tmpi", "tmpi2", "tmpi3", "tmpi4", "tmpi5"])
        S.op("dve", lambda e: e.tensor_single_scalar(out=masks[:, 0, :], in_=tmpi[:, :], scalar=0, op=ALU.is_lt), reads=["tmpi"], writes=["mask0"])
        S.op("dve", lambda e: e.tensor_single_scalar(out=masks[:, 1, :], in_=tmpi[:, :], scalar=0, op=ALU.is_ge), reads=["tmpi"], writes=["mask1"])
        S.op("dve", lambda e: e.tensor_scalar(out=masks[:, 2, :], in0=masks[:, 0, :], scalar1=metab[:, 1:2], scalar2=None, op0=ALU.mult), reads=["mask0", "metab"], writes=["mask2"])
        S.op("dve", lambda e: e.tensor_single_scalar(out=masks[:, 3, :], in_=tmpi[:, :], scalar=0, op=ALU.is_ge), reads=["tmpi"], writes=["mask2"])
        S.op("act", lambda e: e.activation(out=ske[:], in_=skb[:], func=AF.Exp), reads=["skb"], writes=["ske"])
        for half in range(2):
            for c in range(2):
                hh = (2 * c + half) * 4
                S.op("pool", (lambda e, half=half, c=c, hh=hh: e.tensor_copy(
                    out=sinkexp[half * 64:(half + 1) * 64, c, :].rearrange("p (g q) -> p g q", q=128),
                    in_=ske[half * 64:(half + 1) * 64, hh:hh + 4].unsqueeze(2).to_broadcast([64, 4, 128]))),
                    reads=["ske"], writes=["sinkexp"])
                S.op("pool", (lambda e, half=half, c=c, hh=hh: e.tensor_copy(
                    out=sk16[half * 64:(half + 1) * 64, :].rearrange("p (b c g) -> p b c g", c=2, g=4)[:, :, c, :],
                    in_=ske[half * 64:(half + 1) * 64, hh:hh + 4].unsqueeze(1).to_broadcast([64, 4, 4]))),
                    reads=["ske"], writes=["sk16"])
        S.dma("sp", cosT[:, :], tabs[0], reads=["tabs"], writes=["cosT"], sem="c2")
        S.dma("sp", sinT[:, :], tabs[1], reads=["tabs"], writes=["sinT"], sem="c2")
        if stop_at == 'tables':
            finish_debug()
            return nc
        slot1 = {"q": 0, "pt": 0, "og": 0}

        def nx1(k):
            v = slot1[k]
            slot1[k] ^= 1
            return v

        def qk_pipeline(descs, N, tcol0, zlist=()):
            HK = ["hT0", "hT1", "hT2", "hT3"]
            st = {}

            def stage_a(n):
                chunk_idx = descs[n][0]
                wv, wkey = ring_take(chunk_idx)
                b = bank()
                mm_group(ps[b][:, 0:N], [(wv[:, kc, :], hT[:, kc, 0:N]) for kc in range(8)], HK + [wkey], f"ps{b}")
                sq = nx1("q")
                S.op("dve", (lambda e, b=b, sq=sq: e.tensor_copy(out=q32[:, sq, 0:N], in_=ps[b][:, 0:N])), reads=[f"ps{b}"], writes=[f"q32_{sq}"])
                S.op("act", (lambda e, sq=sq: e.activation(out=qsq[:, sq, 0:N], in_=q32[:, sq, 0:N], func=AF.Square)), reads=[f"q32_{sq}"], writes=[f"qsq{sq}"])
                st[n] = sq

            def stage_b(n):
                sq = st[n]
                gcol = descs[n][3]
                sn = n % 2
                bs = bank()
                mm_group(ps[bs][:, 0:N], [(blk64[:, :], qsq[:, sq, 0:N])], [f"qsq{sq}", "blk64"], f"ps{bs}")
                S.op("act", (lambda e, bs=bs: e.activation(out=rs[:, 0, 0:N], in_=ps[bs][:, 0:N], func=AF.Ln, scale=1.0 / 64, bias=epsb[:, 0:1])), reads=[f"ps{bs}", "epsb"], writes=["rs"])
                S.op("act", (lambda e: e.activation(out=rs[:, 0, 0:N], in_=rs[:, 0, 0:N], func=AF.Exp, scale=-0.5)), reads=["rs"], writes=["rs"])
                S.op("dve", (lambda e, sq=sq, sn=sn, gcol=gcol: e.scalar_tensor_tensor(out=qn[:, sn, 0:N], in0=q32[:, sq, 0:N], scalar=gq2[:, gcol:gcol + 1], in1=rs[:, 0, 0:N], op0=ALU.mult, op1=ALU.mult)),
                     reads=[f"q32_{sq}", "rs", "gq2"], writes=[f"qn{sn}"])

            def stage_c(n):
                _, out_ap, out_key, gcol, f32_out = descs[n]
                sn = n % 2
                br = bank()
                mm_group(ps[br][:, 0:N], [(perm16[:, :], qn[:, sn, 0:N])], [f"qn{sn}", "perm16"], f"ps{br}")
                S.op("pool", (lambda e, sn=sn: e.tensor_tensor(out=t1[:, 0, 0:N], in0=qn[:, sn, 0:N], in1=cosT[:, tcol0:tcol0 + N], op=ALU.mult)), reads=[f"qn{sn}", "cosT"], writes=["t1"])
                S.op("dve", (lambda e, br=br: e.tensor_tensor(out=t2[:, 0, 0:N], in0=ps[br][:, 0:N], in1=sinT[:, tcol0:tcol0 + N], op=ALU.mult)), reads=[f"ps{br}", "sinT"], writes=["t2"])
                for (lo_, hi_, oap) in out_ap:
                    S.op("pool", (lambda e, oap=oap, lo_=lo_, hi_=hi_: e.tensor_tensor(out=oap, in0=t1[lo_:hi_, 0, 0:N], in1=t2[lo_:hi_, 0, 0:N], op=ALU.add)), reads=["t1", "t2"], writes=[out_key])
                if f32_out is not None:
                    fo, fkey, lo = f32_out
                    S.op("pool", (lambda e, fo=fo, lo=lo: e.tensor_tensor(out=fo, in0=t1[:, 0, lo:N], in1=t2[:, 0, lo:N], op=ALU.add)), reads=["t1", "t2"], writes=[fkey])

            nd = len(descs)
            zl = list(zlist)
            for n in range(nd + 2):
                if n < nd:
                    stage_a(n)
                if 0 <= n - 1 < nd:
                    stage_b(n - 1)
                if 0 <= n - 2 < nd:
                    stage_c(n - 2)
                if n < nd and descs[n][0] < 8 and zl:
                    z_chunk(zl.pop(0), N)
            for zc in zl:
                z_chunk(zc, N)
            z_flush()

        def v_chunks(tiles, vslots, f32_last=False):
            wv0, k0 = ring_take(10)
            wv1, k1 = ring_take(11)
            for ti_, ((src, key, ntok, col0), vs_) in enumerate(zip(tiles, vslots)):
                b = bank()
                for vc, (wv, wk) in enumerate(((wv0, k0), (wv1, k1))):
                    mm_group(ps[b][0:ntok, vc * 128:(vc + 1) * 128], [(hT[:, kc, col0:col0 + ntok], wv[:, kc, :]) for kc in range(8)],
                             ["hT0", "hT1", "hT2", "hT3", wk], f"ps{b}")
                for half in range(2):
                    S.op("act", (lambda e, b=b, ntok=ntok, vs_=vs_, half=half: e.activation(
                        out=vTp[0:ntok, vs_, :, :].rearrange("p (c hh) x -> p c hh x", hh=2)[:, :, half, half * 64:(half + 1) * 64],
                        in_=ps[b][0:ntok, 0:256].rearrange("p (c hh d) -> p c hh d", hh=2, d=64)[:, :, half, :], func=AF.Copy)),
                        reads=[f"ps{b}"], writes=[f"vT{vs_}"])
                if f32_last and ti_ == len(tiles) - 1:
                    S.op("act", (lambda e, b=b, ntok=ntok: e.activation(out=vf32[0:ntok, :], in_=ps[b][0:ntok, 0:256], func=AF.Copy)), reads=[f"ps{b}"], writes=["vf32"])

        zpend = [None]

        def z_flush():
            if zpend[0] is not None:
                zpend[0]()
                zpend[0] = None

        def z_chunk(zc, N):
            wv, wkey = ring_take(12 + zc)
            b = bank(hold=True)
            zsl = zc % 2
            ztv = zt[:, 0, 0:N] if zsl == 0 else xpre[:, 512:512 + N]
            zk = f"zt{zsl}"
            mm_group(ps[b][:, 0:N], [(wv[:, kc, :], hT[:, kc, 0:N]) for kc in range(8)], ["hT0", "hT1", "hT2", "hT3", wkey], f"ps{b}")
            S.op("act", (lambda e: e.activation(out=ztv, in_=ps[b][:, 0:N], func=AF.Exp, scale=-1.0)), reads=[f"ps{b}"], writes=[zk])
            S.op("act", (lambda e: e.activation(out=ztv, in_=ztv, func=AF.Ln, bias=oneb[:, 0:1])), reads=[zk, "oneb"], writes=[zk])
            S.op("act", (lambda e: e.activation(out=ztv, in_=ztv, func=AF.Exp, scale=-1.0)), reads=[zk], writes=[zk])
            z_flush()

            def fin():
                S.op("dve", (lambda e: e.tensor_tensor(out=zs[:, zc, 0:N], in0=ps[b][:, 0:N], in1=ztv, op=ALU.mult)), reads=[f"ps{b}", zk], writes=[f"zs{zc}"])
                release(b)

            zpend[0] = fin

        qkeys = [f"qT{i}" for i in range(8)]

        def attention_stream(m, tl, ntl):
            LA = 1
            units = [(j, c, half) for j in range(4) for c in range(2) for half in range(2)]
            acc_banks = {}
            deferred = []

            PTpairs = [PT, xpre[:, :].bitcast(BF16)[:, 0:1024].rearrange("p (a b) -> p a b", b=512)]
            ucount = [0]

            def scores(j, c, half):
                qc0 = 128 * j
                pair = bank_pair()
                for kb in range(2):
                    b = pair + kb
                    kc0 = qc0 + 128 * kb
                    S.op("pe", (lambda e, b=b, c=c, half=half, kc0=kc0, qc0=qc0: e.matmul(
                        ps[b].rearrange("p (g q) -> p g q", q=128),
                        lhsT=kTp[:, c, half, kc0:kc0 + 128],
                        rhs=qT[:, 4 * c:4 * c + 4, qc0:qc0 + 128], start=True, stop=True)),
                        reads=[f"kT{c}"] + qkeys, writes=[f"ps{b}"], signal=(kb == 1))
                return pair

            def estage(j, c, half, pair):
                first = (m == 0 and j == 0)
                pp = ucount[0] % 2
                ucount[0] += 1
                PTp = PTpairs[pp]
                pk = f"PTp{pp}"
                mk0 = 2 if first else 0
                S.op("act", (lambda e, pair=pair, PTp=PTp: e.activation(out=PTp.rearrange("p a b -> p (a b)"), in_=psbig[:, pair * 512:(pair + 2) * 512], func=AF.Exp, scale=0.125)),
                     reads=[f"ps{pair}", f"ps{pair + 1}"], writes=[pk])
                release(pair)
                release(pair + 1)
                S.op("dve", (lambda e, PTp=PTp, mk0=mk0: e.tensor_tensor(out=PTp, in0=PTp, in1=masks[:, mk0:mk0 + 2, :], op=ALU.mult)),
                     reads=[pk, "mask0", "mask1", "mask2"], writes=[pk])
                return PTp, pk

            def pstage(j, c, half, PTp, pk):
                h = 2 * c + half
                if half == 0:
                    ab = bank_pair()
                    acc_banks[(j, c)] = (ab, ab + 1)
                bo, bd = acc_banks[(j, c)]
                for kb in range(2):
                    vslot = j + kb
                    S.op("pe", (lambda e, bo=bo, h=h, vslot=vslot, PTp=PTp, kb=kb, half=half: e.matmul(
                        ps[bo], lhsT=vTp[:, vslot, h, :], rhs=PTp[:, kb, :], start=(kb == 0 and half == 0), stop=(kb == 1 and half == 1))),
                        reads=[f"vT{vslot}", pk], writes=[f"ps{bo}"], signal=False)
                for kb in range(2):
                    S.op("pe", (lambda e, bd=bd, PTp=PTp, kb=kb, half=half: e.matmul(
                        ps[bd], lhsT=onesp[:, half, :], rhs=PTp[:, kb, :], start=(kb == 0 and half == 0), stop=(kb == 1 and half == 1))),
                        reads=["onesp", pk], writes=[f"ps{bd}"], signal=(kb == 1))

            def norm_a(j, c):
                bo, bd = acc_banks[(j, c)]
                S.op("dve", (lambda e, bd=bd, c=c: e.tensor_tensor(out=rr[:, 0, :], in0=ps[bd][:, :], in1=sinkexp[:, c, :], op=ALU.add)), reads=[f"ps{bd}", "sinkexp"], writes=["rr"])

            def norm_a2(j, c):
                S.op("act", (lambda e: e.activation(out=rr[:, 0, :], in_=rr[:, 0, :], func=AF.Ln)), reads=["rr"], writes=["rr"])
                S.op("act", (lambda e: e.activation(out=rr[:, 0, :], in_=rr[:, 0, :], func=AF.Exp, scale=-1.0)), reads=["rr"], writes=["rr"])

            def norm_b(j, c):
                bo, bd = acc_banks[(j, c)]
                so = j % 2
                qc0 = 128 * j
                S.op("dve", (lambda e, bo=bo: e.tensor_tensor(out=ogt[:, 0, :], in0=ps[bo][:, :], in1=rr[:, 0, :], op=ALU.mult)), reads=[f"ps{bo}", "rr"], writes=["ogt"])
                S.op("pool", (lambda e, c=c, so=so, qc0=qc0: e.tensor_tensor(
                    out=og[:, so, :].rearrange("p (k q) -> p k q", q=128)[:, 4 * c:4 * c + 4, :],
                    in0=ogt[:, 0, :].rearrange("p (g q) -> p g q", q=128),
                    in1=zs[:, 4 * c:4 * c + 4, qc0:qc0 + 128], op=ALU.mult)),
                    reads=["ogt"] + [f"zs{4 * c + g}" for g in range(4)], writes=[f"og{so}"])
                release(bo)
                release(bd)

            def finish_tile(j, hs_):
                xap, xkey = tl[j][0], tl[j][1]
                if ntl is not None:
                    norm_post(ntl[j], j, hs_)
                out_proj(j % 2, 128, xap, xkey)
                t = 1 + 4 * m + j
                if j % 2 == 1:
                    S.dma("sp", y_d[128 * (t - 2):128 * t, :].rearrange("(a p) f -> p a f", p=128), x1s[:, t - 1:t + 1, :],
                          reads=[tl[j - 1][1], xkey], sem="outs2")

            nu = len(units)
            pend_s = {}
            pend_e = {}
            hs_of = {}
            if ntl is not None:
                norm_stats(ntl)
            pend_s[0] = scores(*units[0])
            pend_s[1] = scores(*units[1])
            pend_e[0] = estage(*units[0], pend_s.pop(0))
            for i, (j, c, half) in enumerate(units):
                if c == 0 and half == 0 and ntl is not None:
                    hs_of[j] = norm_pre(ntl[j], j)
                if i + 2 < nu:
                    pend_s[i + 2] = scores(*units[i + 2])
                if i + 1 < nu:
                    pend_e[i + 1] = estage(*units[i + 1], pend_s.pop(i + 1))
                pstage(j, c, half, *pend_e.pop(i))
                for d in sorted([d for d in deferred if d[0] <= i], key=lambda d: d[0]):
                    deferred.remove(d)
                    d[1]()
                if half == 1:
                    deferred.append((i + 1, (lambda j=j, c=c: norm_a(j, c))))
                    deferred.append((i + 2, (lambda j=j, c=c: norm_a2(j, c))))
                    deferred.append((i + 3, (lambda j=j, c=c: norm_b(j, c))))
                    if c == 1:
                        deferred.append((i + 4, (lambda j=j: finish_tile(j, hs_of.get(j)))))
            for d in sorted(deferred, key=lambda d: d[0]):
                d[1]()

        def out_proj(so, ntok, xap, xkey):
            ogv = og[:, so, :].rearrange("p (k q) -> p k q", q=128)
            bs2 = []
            for n2 in range(2):
                b = bank(hold=True)
                bs2.append(b)
                mm_group(ps[b][0:ntok, :], [(ogv[:, ch, 0:ntok], wO[:, ch, n2 * 512:(n2 + 1) * 512]) for ch in range(8)],
                         [f"og{so}"] + [f"wO{ch}" for ch in range(8)], f"ps{b}")
            for n2, b in enumerate(bs2):
                S.op("dve", (lambda e, b=b, n2=n2: e.tensor_tensor(out=xap[:, n2 * 512:(n2 + 1) * 512], in0=ps[b][0:ntok, :], in1=xap[:, n2 * 512:(n2 + 1) * 512], op=ALU.add)),
                     reads=[f"ps{b}", xkey], writes=[xkey])
                release(b)

        norm_stage([(x1s[:, 0, :], "x1_0", 128, 0)])
        v_chunks([(x1s[:, 0, :], "x1_0", 128, 0)], [0])
        qk_pipeline([(8 + kc_, [(0, 64, kTp[0:64, kc_, 0, 0:128]), (64, 128, kTp[64:128, kc_, 1, 0:128])], f"kT{kc_}", 1, None) for kc_ in range(2)], 128, 16)

        if stop_at == 'halo':
            finish_debug()
            return nc
        def main_tiles(m):
            return [(x1s[:, 1 + 4 * m + j, :], f"x1_{1 + 4 * m + j}", 128, 128 * j) for j in range(4)]

        for m in range(NMT):
            tl = main_tiles(m)
            last = (m == NMT - 1)
            if m == 0:
                norm_stage(tl)
            tcol = 144 + 512 * m
            v_chunks(tl, [1, 2, 3, 4], f32_last=last)
            descs = [(8 + kc_, [(0, 64, kTp[0:64, kc_, 0, 128:640]), (64, 128, kTp[64:128, kc_, 1, 128:640])], f"kT{kc_}", 1, ((kf32[:, kc_, :], "kf32", 384) if last else None)) for kc_ in range(2)]
            descs += [(qc, [(0, 128, qT[:, qc, :])], f"qT{qc}", 0, None) for qc in range(8)]
            qk_pipeline(descs, 512, tcol, zlist=range(8))
            ntl = main_tiles(m + 1) if not last else None
            attention_stream(m, tl, ntl)
            if not last:
                for kc_ in range(2):
                    S.op("pool", (lambda e, kc_=kc_: e.tensor_copy(out=kTp[:, kc_, :, 0:128], in_=kTp[:, kc_, :, 512:640])), reads=[f"kT{kc_}"], writes=[f"kT{kc_}"])
                S.op("pool", (lambda e: e.tensor_copy(out=vTp[:, 0, :, :], in_=vTp[:, 4, :, :])), reads=["vT4", "vT0"], writes=["vT0"])

        if stop_at == 'main':
            finish_debug()
            return nc
        for kc_, tbuf, tkey in ((0, t1, "t1"), (1, t2, "t2")):
            bt = bank()
            for q4 in range(4):
                S.op("pe", (lambda e, kc_=kc_, q4=q4, bt=bt: e.transpose(ps[bt][0:32, q4 * 128:(q4 + 1) * 128], kf32[:, kc_, 32 * q4:32 * q4 + 32], idf[:, :])),
                     reads=["kf32", "idf"], writes=[f"ps{bt}"], signal=(q4 == 3))
            S.op("act", (lambda e, bt=bt, tbuf=tbuf: e.activation(out=tbuf[0:32, 0, :], in_=ps[bt][0:32, :], func=AF.Copy)), reads=[f"ps{bt}"], writes=[tkey])
            S.dma("pool", nkp_d[:, kc_ * 128:(kc_ + 1) * 128].rearrange("(q p) f -> p q f", p=32),
                  tbuf[0:32, 0, :].rearrange("p (q f) -> p q f", f=128), reads=[tkey], sem=f"zo_k{kc_}")
        if stop_at == 'pkvB':
            finish_debug()
            return nc
        S.dma("pool", nvp_d[:, :], vf32[:, :], reads=["vf32"], sem="zo_v")

        if stop_at == 'pkv':
            finish_debug()
            return nc
        TS = NS
        norm_stage([(xsm[0:16, :], "xsm", 16, 0)])
        v_chunks([(xsm[0:16, :], "xsm", 16, 0)], [1], f32_last=True)
        descs = [(8 + kc_, [(0, 64, kTp[0:64, kc_, 0, 128:144]), (64, 128, kTp[64:128, kc_, 1, 128:144])], f"kT{kc_}", 1, (kf32[:, kc_, 0:16], "kf32", 0)) for kc_ in range(2)]
        descs += [(qc, [(0, 128, qT[:, qc, 0:16])], f"qT{qc}", 0, None) for qc in range(8)]
        qk_pipeline(descs, 16, TS, zlist=range(8))
        if stop_at == 'sA':
            finish_debug()
            return nc
        bt = bank()
        for kc_ in range(2):
            S.op("pe", (lambda e, kc_=kc_, bt=bt: e.transpose(ps[bt][0:16, kc_ * 128:(kc_ + 1) * 128], kf32[:, kc_, 0:16], idf[:, :])), reads=["kf32", "idf"], writes=[f"ps{bt}"], signal=(kc_ == 1))
        S.op("act", lambda e, bt=bt: e.activation(out=t2[0:16, 0, 0:256], in_=ps[bt][0:16, 0:256], func=AF.Copy), reads=[f"ps{bt}"], writes=["t2"])
        S.dma("pool", nks_d[:, 127, :], t2[0:16, 0, 0:256], reads=["t2"], sem="outs")
        S.dma("pool", nvs_d[:, 127, :], vf32[0:16, :], reads=["vf32"], sem="outs")
        S.dma("sp", nks_d[:, 0:127, :], ck_d[:, 1:128, :], sem="outs2")
        S.dma("sp", nvs_d[:, 0:127, :], cv_d[:, 1:128, :], sem="outs2")
        S.op("act", lambda e: e.activation(out=vsb[0:16, :], in_=vf32[0:16, :], func=AF.Copy), reads=["vf32"], writes=["vsb"])
        if stop_at == 'sB':
            finish_debug()
            return nc
        ogs = og[:, 0, :].rearrange("p (k q) -> p k q", q=128)
        for gi in range(4):
            b0 = 4 * gi
            ckv = wr[:, 0:2, :].rearrange("p a b -> p (a b)").bitcast(F32).rearrange("p (b f) -> p b f", f=256)
            cvv = wr[:, 2:4, :].rearrange("p a b -> p (a b)").bitcast(F32).rearrange("p (b f) -> p b f", f=256)
            S.dma("sp", ckv, ck_d[b0:b0 + 4].rearrange("b k f -> k b f"), writes=["wr0", "wr1"], sem="wr0")
            S.dma("sp", cvv, cv_d[b0:b0 + 4].rearrange("b k f -> k b f"), writes=["wr2", "wr3"], sem="wr2")
            S.op("dve", lambda e: e.tensor_copy(out=ckb, in_=ckv), reads=["wr0", "wr1"], writes=["ckb", "mask0", "mask1", "mask2", "sinkexp"])
            S.op("act", lambda e: e.activation(out=vsg, in_=cvv, func=AF.Copy), reads=["wr2", "wr3"], writes=["vsg", "mask0", "mask1", "mask2", "sinkexp"])
            S.dma("sp", vsg[0:1, :, :], vsb[b0:b0 + 4, :], reads=["vsb", "vsg"], writes=["vsg"], sem="vsg")
            if stop_at == 'g0a':
                finish_debug()
                return nc
            bt = bank()
            pst = ps[bt][:, :].bitcast(BF16).rearrange("p (k t) -> p k t", t=128)
            for bi in range(4):
                for c in range(2):
                    S.op("pe", (lambda e, bi=bi, c=c, pst=pst: e.transpose(pst[:, 2 * bi + c, :], ckb[:, bi, c * 128:(c + 1) * 128], id16[:, :])),
                         reads=["ckb", "id16"], writes=[f"ps{bt}"], signal=(bi == 3 and c == 1))
            S.op("act", (lambda e, pst=pst: e.activation(out=ktg, in_=pst, func=AF.Copy)), reads=[f"ps{bt}"], writes=["ktg", "mask0", "mask1", "mask2", "sinkexp"])
            for half in range(2):
                S.op("dve", (lambda e, b0=b0, half=half: e.tensor_copy(
                    out=ktg.rearrange("p (b c) k -> p c b k", c=2)[half * 64:(half + 1) * 64, :, :, 0],
                    in_=kTp[half * 64:(half + 1) * 64, :, half, 128 + b0:128 + b0 + 4])),
                    reads=["ktg", "kT0", "kT1"], writes=["ktg"])
            if stop_at == 'g0b':
                finish_debug()
                return nc
            bsb2 = [bank(), bank()]
            for half in range(2):
                for bi in range(4):
                    for c in range(2):
                        last_ = (bi == 3 and c == 1)
                        S.op("pe", (lambda e, bi=bi, c=c, half=half, b0=b0, bsb2=bsb2: e.matmul(
                            ps[bsb2[half]][:, (bi * 2 + c) * 4:(bi * 2 + c) * 4 + 4], lhsT=ktg[half * 64:(half + 1) * 64, 2 * bi + c, :],
                            rhs=qT[half * 64:(half + 1) * 64, 4 * c:4 * c + 4, b0 + bi], start=True, stop=True)),
                            reads=["ktg"] + qkeys, writes=[f"ps{bsb2[half]}"], signal=last_)
            for half in range(2):
                S.op("act", (lambda e, half=half, bsb2=bsb2: e.activation(
                    out=pts[:, 0:64].rearrange("p (b c h g) -> p b c h g", c=2, h=2, g=4)[:, :, :, half, :],
                    in_=ps[bsb2[half]][:, 0:32].rearrange("p (b c g) -> p b c g", c=2, g=4), func=AF.Exp, scale=0.125)),
                    reads=[f"ps{bsb2[half]}"], writes=["pts"])
            if stop_at == 'g0c':
                finish_debug()
                return nc
            bdn = bank()
            mm_group(ps[bdn][:, 0:64], [(ones16[:, :], pts[:, 0:64])], ["pts", "ones16"], f"ps{bdn}")
            bos = bank()
            for bi in range(4):
                for c in range(2):
                    for half in range(2):
                        h = 2 * c + half
                        last_ = (bi == 3 and c == 1 and half == 1)
                        S.op("pe", (lambda e, bi=bi, c=c, half=half, h=h, bos=bos: e.matmul(
                            ps[bos][half * 64:(half + 1) * 64, bi * 8 + c * 4:bi * 8 + c * 4 + 4], lhsT=vsg[:, bi, h * 64:(h + 1) * 64],
                            rhs=pts[:, (bi * 4 + h) * 4:(bi * 4 + h) * 4 + 4], start=True, stop=True)),
                            reads=["vsg", "pts"], writes=[f"ps{bos}"], signal=last_)
            if stop_at == 'g0d':
                finish_debug()
                return nc
            for half in range(2):
                S.op("dve", (lambda e, half=half, bdn=bdn: e.tensor_tensor(
                    out=rsm[half * 64:(half + 1) * 64, :].rearrange("p (b c g) -> p b c g", c=2, g=4),
                    in0=ps[bdn][half * 64:(half + 1) * 64, 0:64].rearrange("p (b c h g) -> p b c h g", c=2, h=2, g=4)[:, :, :, half, :],
                    in1=sk16[half * 64:(half + 1) * 64, :].rearrange("p (b c g) -> p b c g", c=2, g=4), op=ALU.add)),
                    reads=[f"ps{bdn}", "sk16"], writes=["rsm"])
            S.op("dve", lambda e: e.reciprocal(out=rsm[:, :], in_=rsm[:, :]), reads=["rsm"], writes=["rsm"])
            S.op("dve", (lambda e, bos=bos: e.tensor_tensor(out=ossm[:, :], in0=ps[bos][:, 0:32], in1=rsm[:, :], op=ALU.mult)), reads=[f"ps{bos}", "rsm"], writes=["ossm"])
            S.op("dve", (lambda e, b0=b0: e.tensor_tensor(
                out=ogs[:, :, b0:b0 + 4], in0=ossm[:, :].rearrange("p (b k) -> p k b", k=8), in1=zs[:, :, b0:b0 + 4], op=ALU.mult)),
                reads=["ossm"] + [f"zs{i}" for i in range(8)], writes=["og0"])
            if stop_at == 'g0e':
                finish_debug()
                return nc
        if stop_at == 'sE':
            finish_debug()
            return nc
        out_proj(0, 16, xsm[0:16, :], "xsm")
        S.dma("pool", ys_d[:, :], xsm[0:16, :], reads=["xsm"], sem="outs")

        S.wait_all("sp", [k for k in S.sems if k == "outs2" or k.startswith("wscr")])
        S.wait_all("pool", ["outs", "wnatA"] + [k for k in S.sems if k.startswith("zo_")])
        S.wait_all("act", ["tabs"])
        S.replay()
    return nc


def _core_inputs(c, NMT, inp):
    NT = 4 * NMT
    L = 128 * NT
    b, j = c // 4, c % 4
    xp = inp["x_prompt"]
    start = j * L
    xin = np.zeros((144 + L, 1024), np.float32)
    if j > 0:
        xin[0:144] = xp[b, start - 144:start]
    xin[144:] = xp[b, start:start + L]
    sl = slice(16 * c, 16 * c + 16)
    d = {
        "xin": xin,
        "xs": np.ascontiguousarray(inp["x_sample"][sl, 0, :]),
        "spool": np.ascontiguousarray(inp["state_pool"][0, sl].reshape(240, 512)),
        "sconv": np.ascontiguousarray(inp["state_conv"][0, sl].reshape(32, 512)),
        "ck": np.ascontiguousarray(inp["cache_k"][0, sl].reshape(16, 128, 256)),
        "cv": np.ascontiguousarray(inp["cache_v"][0, sl].reshape(16, 128, 256)),
        "norm_g": np.ascontiguousarray(inp["norm_g"]),
        "w_in_even": np.ascontiguousarray(inp["w_in_even"][0]),
        "pool_w": np.ascontiguousarray(inp["pool_w"][0]),
        "pool_scale": np.ascontiguousarray(inp["pool_scale"][0:1]),
        "conv_w": np.ascontiguousarray(inp["conv_w"][0]),
        "w_out_even": np.ascontiguousarray(inp["w_out_even"][0]),
        "w_in_odd": np.ascontiguousarray(inp["w_in_odd"][0]),
        "q_norm_g": np.ascontiguousarray(inp["q_norm_g"][0:1]),
        "k_norm_g": np.ascontiguousarray(inp["k_norm_g"][0:1]),
        "attn_sinks": np.ascontiguousarray(inp["attn_sinks"][0:1]),
        "w_out_odd": np.ascontiguousarray(inp["w_out_odd"][0]),
        "meta": np.array([[float(start), 1.0 if j > 0 else 0.0]], np.float32),
    }
    return {k: np.asarray(v, np.float32) for k, v in d.items()}


def kernel(**inputs):
    NMT = 4
    inp = {k: np.asarray(v) for k, v in inputs.items()}
    nc = build(NMT=NMT)
    in_maps = [_core_inputs(c, NMT, inp) for c in range(8)]
    res = run_bass_kernel_spmd(nc, in_maps, core_ids=list(range(8)))
    r = res.results
    L = 512 * NMT
    y = np.zeros((2, 4 * L, 1024), np.float32)
    for c in range(8):
        y[c // 4, (c % 4) * L:(c % 4 + 1) * L] = r[c]["y"]
    ys = np.concatenate([r[c]["ys"] for c in range(8)], 0).reshape(128, 1, 1024)
    npp = np.stack([r[3]["npp"], r[7]["npp"]])[None]
    nps = np.concatenate([r[c]["nps"] for c in range(8)], 0)[None]
    ncp = np.stack([r[3]["ncp"], r[7]["ncp"]])[None]
    ncs = np.concatenate([r[c]["ncs"] for c in range(8)], 0)[None]
    nkp = np.stack([r[3]["nkp"], r[7]["nkp"]]).reshape(1, 2, 128, 4, 64)
    nvp = np.stack([r[3]["nvp"], r[7]["nvp"]]).reshape(1, 2, 128, 4, 64)
    nks = np.concatenate([r[c]["nks"] for c in range(8)], 0).reshape(1, 128, 128, 4, 64)
    nvs = np.concatenate([r[c]["nvs"] for c in range(8)], 0).reshape(1, 128, 128, 4, 64)
    f = lambda a: np.ascontiguousarray(a, dtype=np.float32)
    return (f(y), f(ys), f(npp), f(nps), f(ncp), f(ncs), f(nkp), f(nvp), f(nks), f(nvs))
```
